# Optimizing a Trainium2 kernel written in Bass

```python
import math
import jax, jax.numpy as jnp
from jax import lax
import numpy as np

D_MODEL = 1024
BATCH = 2
SEQ = 16384
DEPTH = 4
DEC_BATCH = 16
DEC_SEQ = 64
PAST_LEN = 4096

CHUNK = 64
N_META = 16
EPS = 1e-6
ATT_HEADS = 4
ATT_DH = 64
ATT_VD = 2 * ATT_DH
ATT_WIDTH = ATT_HEADS * ATT_VD
Q_BLOCK = 128
LRU_WIDTH = 256
LRU_BLOCKS = 8
LRU_BW = LRU_WIDTH // LRU_BLOCKS
LRU_CONV = 4
LRU_C = 8.0
GLA_HEADS = 4
GLA_DK = 32
GLA_DV = 64
GLA_KW = GLA_HEADS * GLA_DK
GLA_VW = GLA_HEADS * GLA_DV
GLA_RANK = 16
GLA_TAU = 16.0
GLA_BLOCK = 64
MIX_WIDTH = ATT_WIDTH + LRU_WIDTH + GLA_VW
IN_SIZES = (ATT_HEADS * 2 * ATT_DH, ATT_HEADS * 2 * ATT_DH, ATT_WIDTH,
            LRU_WIDTH, LRU_WIDTH,
            GLA_KW, GLA_KW, GLA_VW, GLA_RANK, GLA_VW)
IN_WIDTH = sum(IN_SIZES)
FFN_DIM = 2816
FFN_CONV = 3

kernel_name = "hymba_style_diffattn_rglru_gla_streaming_step"


def rms_norm(x, g):
    xf = x.astype(jnp.float32)
    y = xf * lax.rsqrt(jnp.mean(xf * xf, axis=-1, keepdims=True) + EPS)
    return (y * g.astype(jnp.float32)).astype(x.dtype)


def causal_dwconv(u, state, w, b):
    width = w.shape[0]
    up = jnp.concatenate([state.astype(u.dtype), u], axis=1)
    y = lax.conv_general_dilated(up, w[:, None, :].astype(u.dtype), window_strides=(1,), padding='VALID',
                                 dimension_numbers=('NWC', 'WIO', 'NWC'), feature_group_count=u.shape[-1])
    return y + b.astype(u.dtype), up[:, up.shape[1] - (width - 1):]


def chunk_id(pos):
    return jnp.floor_divide(pos - N_META, CHUNK)


def diff_softmax_mix(q, k, v, lam, mask):
    s = jnp.einsum('bqhcd,bkhcd->bhcqk', q, k).astype(jnp.float32) * (ATT_DH ** -0.5)
    if mask is not None:
        s = jnp.where(mask, s, -jnp.inf)
    p = jax.nn.softmax(s, axis=-1)
    pd = p[:, :, 0] - lam * p[:, :, 1]
    return jnp.einsum('bhqk,bkhe->bqhe', pd.astype(v.dtype), v)


def diff_attention_prompt(q, k, v, lam):
    b, L = q.shape[:2]
    n_blk = -(-L // Q_BLOCK)
    pad = n_blk * Q_BLOCK - L
    qp = jnp.pad(q, ((0, 0), (0, pad), (0, 0), (0, 0), (0, 0)))
    qb = qp.reshape(b, n_blk, Q_BLOCK, ATT_HEADS, 2, ATT_DH).transpose(1, 0, 2, 3, 4, 5)
    k_chunk = chunk_id(jnp.arange(L))
    starts = jnp.arange(n_blk) * Q_BLOCK

    def one_block(args):
        qblk, start = args
        q_chunk = chunk_id(start + jnp.arange(Q_BLOCK))
        mask = k_chunk[None, :] <= q_chunk[:, None]
        return diff_softmax_mix(qblk, k, v, lam, mask)

    o = lax.map(one_block, (qb, starts))
    return o.transpose(1, 0, 2, 3, 4).reshape(b, n_blk * Q_BLOCK, ATT_HEADS, ATT_VD)[:, :L]


def rg_lru(x, h0, gate_a_w, gate_a_b, gate_x_w, gate_x_b, log_lambda):
    b, L, _ = x.shape
    xb = x.reshape(b, L, LRU_BLOCKS, LRU_BW)
    r = jax.nn.sigmoid((jnp.einsum('blnc,ncd->blnd', xb, gate_a_w).reshape(b, L, LRU_WIDTH)
                        + gate_a_b).astype(jnp.float32))
    i = jax.nn.sigmoid((jnp.einsum('blnc,ncd->blnd', xb, gate_x_w).reshape(b, L, LRU_WIDTH)
                        + gate_x_b).astype(jnp.float32))
    log_a = -LRU_C * r * jax.nn.softplus(-log_lambda.astype(jnp.float32))
    a = jnp.exp(log_a)
    u = jnp.sqrt(-jnp.expm1(2.0 * log_a)) * i * x.astype(jnp.float32)

    def combine(e1, e2):
        a1, b1 = e1
        a2, b2 = e2
        return a1 * a2, a2 * b1 + b2

    A, H = lax.associative_scan(combine, (a, u), axis=1)
    h = H + A * h0.astype(jnp.float32)[:, None, :]
    return h.astype(x.dtype), h[:, -1].astype(x.dtype)


def gla_blocked(q, k, v, log_g, S0):
    b, L = q.shape[:2]
    n_blk = -(-L // GLA_BLOCK)
    pad = n_blk * GLA_BLOCK - L

    def blocks(t):
        t = jnp.pad(t.astype(jnp.float32), ((0, 0), (0, pad), (0, 0), (0, 0)))
        return t.reshape(b, n_blk, GLA_BLOCK, GLA_HEADS, t.shape[-1]).transpose(1, 0, 3, 2, 4)

    qs = blocks(q * (GLA_DK ** -0.5))
    ks, vs, gs = blocks(k), blocks(v), blocks(log_g)
    causal = jnp.tril(jnp.ones((GLA_BLOCK, GLA_BLOCK), bool))

    def step(S, blk):
        qc, kc, vc, gc = blk
        bcum = jnp.cumsum(gc, axis=2)
        o_inter = jnp.einsum('bhcd,bhde->bhce', qc * jnp.exp(bcum), S)
        rel = bcum[:, :, :, None, :] - bcum[:, :, None, :, :]
        decay = jnp.exp(jnp.where(causal[:, :, None], rel, -jnp.inf))
        att = jnp.einsum('bhid,bhjd,bhijd->bhij', qc, kc, decay)
        o = o_inter + jnp.einsum('bhij,bhje->bhie', att, vc)
        b_last = bcum[:, :, -1:, :]
        S_new = (jnp.exp(b_last[:, :, 0, :])[..., None] * S
                 + jnp.einsum('bhcd,bhce->bhde', kc * jnp.exp(b_last - bcum), vc))
        return S_new, o

    S, o = lax.scan(step, S0.astype(jnp.float32), (qs, ks, vs, gs))
    o = o.transpose(1, 0, 3, 2, 4).reshape(b, n_blk * GLA_BLOCK, GLA_HEADS, GLA_DV)[:, :L]
    return o, S


def run_trunk(x, prm, states, is_prompt):
    b, L, _ = x.shape
    dt = x.dtype
    split_points = np.cumsum(IN_SIZES)[:-1].tolist()
    new_k, new_v, new_lh, new_lc, new_S, new_fc = [], [], [], [], [], []
    for l in range(DEPTH):
        if is_prompt:
            lru_h0 = jnp.zeros((b, LRU_WIDTH), dt)
            lru_c0 = jnp.zeros((b, LRU_CONV - 1, LRU_WIDTH), dt)
            S0 = jnp.zeros((b, GLA_HEADS, GLA_DK, GLA_DV), dt)
            ffn_c0 = jnp.zeros((b, FFN_CONV - 1, FFN_DIM), dt)
        else:
            ck, cv = states[0][l], states[1][l]
            lru_h0, lru_c0, S0, ffn_c0 = states[2][l], states[3][l], states[4][l], states[5][l]

        hn = rms_norm(x, prm['norm_mix_g'][l])
        proj = hn @ prm['w_in'][l]
        aq, ak, av, lx, lg, gq, gk, gv, glr, gog = jnp.split(proj, split_points, axis=-1)

        q = aq.reshape(b, L, ATT_HEADS, 2, ATT_DH)
        k = ak.reshape(b, L, ATT_HEADS, 2 * ATT_DH)
        v = av.reshape(b, L, ATT_HEADS, ATT_VD)
        lam_init = 0.8 - 0.6 * math.exp(-0.3 * l)
        lqk = prm['attn_lambda'][l].astype(jnp.float32)
        lam = jnp.exp(jnp.sum(lqk[0] * lqk[1])) - jnp.exp(jnp.sum(lqk[2] * lqk[3])) + lam_init
        if is_prompt:
            o_att = diff_attention_prompt(q, k.reshape(b, L, ATT_HEADS, 2, ATT_DH), v, lam)
        else:
            k_all = jnp.concatenate([ck.astype(dt), k], axis=1)
            v_all = jnp.concatenate([cv.astype(dt), v], axis=1)
            o_att = diff_softmax_mix(q, k_all.reshape(b, k_all.shape[1], ATT_HEADS, 2, ATT_DH), v_all, lam, None)
        o_att = (rms_norm(o_att, prm['attn_subln_g'][l]) * (1.0 - lam_init)).reshape(b, L, ATT_WIDTH)

        xc, lru_c1 = causal_dwconv(lx, lru_c0, prm['lru_conv_w'][l], prm['lru_conv_b'][l])
        hseq, lru_h1 = rg_lru(xc, lru_h0, prm['lru_gate_a_w'][l], prm['lru_gate_a_b'][l],
                              prm['lru_gate_x_w'][l], prm['lru_gate_x_b'][l], prm['lru_log_lambda'][l])
        o_lru = hseq * jax.nn.gelu(lg)

        log_g = jax.nn.log_sigmoid((glr @ prm['gla_gate_w2'][l] + prm['gla_gate_b'][l]).astype(jnp.float32)) / GLA_TAU
        o_gla, S1 = gla_blocked(gq.reshape(b, L, GLA_HEADS, GLA_DK), gk.reshape(b, L, GLA_HEADS, GLA_DK),
                                gv.reshape(b, L, GLA_HEADS, GLA_DV), log_g.reshape(b, L, GLA_HEADS, GLA_DK), S0)
        o_gla = rms_norm(o_gla.astype(dt), prm['gla_norm_g'][l]).reshape(b, L, GLA_VW) * jax.nn.silu(gog)

        x = x + jnp.concatenate([o_att, o_lru, o_gla], axis=-1) @ prm['w_out'][l]

        hn = rms_norm(x, prm['norm_ffn_g'][l])
        u = hn @ prm['ffn_w_up'][l]
        uc, ffn_c1 = causal_dwconv(u, ffn_c0, prm['ffn_conv_w'][l], prm['ffn_conv_b'][l])
        x = x + (jax.nn.gelu(uc) * (hn @ prm['ffn_w_gate'][l])) @ prm['ffn_w_down'][l]

        new_k.append(k)
        new_v.append(v)
        new_lh.append(lru_h1)
        new_lc.append(lru_c1)
        new_S.append(S1.astype(dt))
        new_fc.append(ffn_c1)
    y = rms_norm(x, prm['norm_final_g'])
    return y, (jnp.stack(new_k), jnp.stack(new_v), jnp.stack(new_lh), jnp.stack(new_lc),
               jnp.stack(new_S), jnp.stack(new_fc))


def setup_inputs(seed: int = 0) -> dict:
    key = jax.random.key(seed)
    ks = jax.random.split(key, 40)
    f32 = jnp.float32

    def nrm(i, shape, scale):
        return jax.random.normal(ks[i], shape, f32) * scale

    a0 = jax.random.uniform(ks[39], (DEPTH, LRU_WIDTH), f32, minval=0.9, maxval=0.999) ** (1.0 / LRU_C)
    return {
        "x_prompt": nrm(0, (BATCH, SEQ, D_MODEL), 1.0),
        "x_sample": nrm(1, (DEC_BATCH, DEC_SEQ, D_MODEL), 1.0),
        "cache_attn_k": nrm(2, (DEPTH, DEC_BATCH, PAST_LEN, ATT_HEADS, 2 * ATT_DH), 1.0),
        "cache_attn_v": nrm(3, (DEPTH, DEC_BATCH, PAST_LEN, ATT_HEADS, ATT_VD), 1.0),
        "state_lru_h": nrm(4, (DEPTH, DEC_BATCH, LRU_WIDTH), 0.5),
        "state_lru_conv": nrm(5, (DEPTH, DEC_BATCH, LRU_CONV - 1, LRU_WIDTH), 1.0),
        "state_gla": nrm(6, (DEPTH, DEC_BATCH, GLA_HEADS, GLA_DK, GLA_DV), 0.5),
        "state_ffn_conv": nrm(7, (DEPTH, DEC_BATCH, FFN_CONV - 1, FFN_DIM), 1.0),
        "meta_tokens": nrm(8, (N_META, D_MODEL), 1.0),
        "norm_mix_g": 1.0 + nrm(9, (DEPTH, D_MODEL), 0.02),
        "w_in": nrm(10, (DEPTH, D_MODEL, IN_WIDTH), D_MODEL ** -0.5),
        "attn_lambda": nrm(11, (DEPTH, 4, ATT_DH), 0.1),
        "attn_subln_g": 1.0 + nrm(12, (DEPTH, ATT_VD), 0.02),
        "lru_conv_w": nrm(13, (DEPTH, LRU_CONV, LRU_WIDTH), LRU_CONV ** -0.5),
        "lru_conv_b": nrm(14, (DEPTH, LRU_WIDTH), 0.02),
        "lru_gate_a_w": nrm(15, (DEPTH, LRU_BLOCKS, LRU_BW, LRU_BW), LRU_BW ** -0.5),
        "lru_gate_a_b": nrm(16, (DEPTH, LRU_WIDTH), 0.02),
        "lru_gate_x_w": nrm(17, (DEPTH, LRU_BLOCKS, LRU_BW, LRU_BW), LRU_BW ** -0.5),
        "lru_gate_x_b": nrm(18, (DEPTH, LRU_WIDTH), 0.02),
        "lru_log_lambda": jnp.log(a0) - jnp.log1p(-a0),
        "gla_gate_w2": nrm(19, (DEPTH, GLA_RANK, GLA_KW), GLA_RANK ** -0.5),
        "gla_gate_b": nrm(20, (DEPTH, GLA_KW), 0.1),
        "gla_norm_g": 1.0 + nrm(21, (DEPTH, GLA_DV), 0.02),
        "w_out": nrm(22, (DEPTH, MIX_WIDTH, D_MODEL), MIX_WIDTH ** -0.5),
        "norm_ffn_g": 1.0 + nrm(23, (DEPTH, D_MODEL), 0.02),
        "ffn_w_up": nrm(24, (DEPTH, D_MODEL, FFN_DIM), D_MODEL ** -0.5),
        "ffn_conv_w": nrm(25, (DEPTH, FFN_CONV, FFN_DIM), FFN_CONV ** -0.5),
        "ffn_conv_b": nrm(26, (DEPTH, FFN_DIM), 0.02),
        "ffn_w_gate": nrm(27, (DEPTH, D_MODEL, FFN_DIM), D_MODEL ** -0.5),
        "ffn_w_down": nrm(28, (DEPTH, FFN_DIM, D_MODEL), FFN_DIM ** -0.5),
        "norm_final_g": 1.0 + nrm(29, (D_MODEL,), 0.02),
    }


def reference(x_prompt, x_sample, cache_attn_k, cache_attn_v, state_lru_h, state_lru_conv, state_gla,
              state_ffn_conv, meta_tokens, norm_mix_g, w_in, attn_lambda, attn_subln_g, lru_conv_w, lru_conv_b,
              lru_gate_a_w, lru_gate_a_b, lru_gate_x_w, lru_gate_x_b, lru_log_lambda, gla_gate_w2, gla_gate_b,
              gla_norm_g, w_out, norm_ffn_g, ffn_w_up, ffn_conv_w, ffn_conv_b, ffn_w_gate, ffn_w_down,
              norm_final_g):
    prm = {
        'norm_mix_g': norm_mix_g, 'w_in': w_in, 'attn_lambda': attn_lambda, 'attn_subln_g': attn_subln_g,
        'lru_conv_w': lru_conv_w, 'lru_conv_b': lru_conv_b, 'lru_gate_a_w': lru_gate_a_w,
        'lru_gate_a_b': lru_gate_a_b, 'lru_gate_x_w': lru_gate_x_w, 'lru_gate_x_b': lru_gate_x_b,
        'lru_log_lambda': lru_log_lambda, 'gla_gate_w2': gla_gate_w2, 'gla_gate_b': gla_gate_b,
        'gla_norm_g': gla_norm_g, 'w_out': w_out, 'norm_ffn_g': norm_ffn_g, 'ffn_w_up': ffn_w_up,
        'ffn_conv_w': ffn_conv_w, 'ffn_conv_b': ffn_conv_b, 'ffn_w_gate': ffn_w_gate,
        'ffn_w_down': ffn_w_down, 'norm_final_g': norm_final_g,
    }
    b = x_prompt.shape[0]
    meta = jnp.broadcast_to(meta_tokens.astype(x_prompt.dtype)[None], (b, N_META, D_MODEL))
    xp = jnp.concatenate([meta, x_prompt], axis=1)
    yp, (k_p, v_p, lh_p, lc_p, S_p, fc_p) = run_trunk(xp, prm, None, True)
    y_prompt = yp[:, N_META:]
    y_sample, (k_s, v_s, lh_s, lc_s, S_s, fc_s) = run_trunk(
        x_sample, prm, (cache_attn_k, cache_attn_v, state_lru_h, state_lru_conv, state_gla, state_ffn_conv), False)
    return (y_prompt, y_sample, k_p, v_p, lh_p, lc_p, S_p, fc_p, k_s, v_s, lh_s, lc_s, S_s, fc_s)
```

```python
import math
import numpy as np
from contextlib import ExitStack
import concourse.bass as bass
import concourse.mybir as mybir
from concourse.bass_utils import run_bass_kernel_spmd

F32 = mybir.dt.float32
BF16 = mybir.dt.bfloat16
ALU = mybir.AluOpType
AF = mybir.ActivationFunctionType
AX = mybir.AxisListType

D = 1024
NM = 16
DS = 64
EPS = 1e-6
INW = 2832
FF = 2816
NP = 379
O_G1, O_G2, O_LCW, O_LCB, O_BA, O_BX, O_LAM, O_GGB, O_GNG, O_SUBG, O_FCW, O_FCB, O_AL = 0, 8, 16, 24, 26, 28, 30, 32, 33, 34, 35, 101, 123


class Buf:
    __slots__ = ('t', 'w', 'r', 'name')

    def __init__(s, t, name):
        s.t = t
        s.w = None
        s.r = {}
        s.name = name

    def __getitem__(s, k):
        return s.t[k]


class Ctx:
    ENG = ('pe', 'act', 'dve', 'pool', 'sp')

    def __init__(s, nc, es):
        s.nc = nc
        s.es = es
        s.ops = {e: [] for e in s.ENG}
        s.cnt = {e: 0 for e in s.ENG}
        s.seen = {e: {} for e in s.ENG}
        s.dcnt = {}
        s.nb = 0
        s.gbar = []

    def sb(s, name, shape, dt):
        s.nb += 1
        return Buf(s.es.enter_context(s.nc.sbuf_tensor(f"{name}_{s.nb}", list(shape), dt)), name)

    def ps(s, name, shape, dt=F32):
        s.nb += 1
        return Buf(s.es.enter_context(s.nc.psum_tensor(f"{name}_{s.nb}", list(shape), dt)), name)

    def dram(s, name, shape, dt, kind="Internal"):
        return Buf(s.nc.dram_tensor(name, list(shape), dt, kind=kind), name)

    def _need(s, eng, ev, waits):
        if ev is None:
            return
        k, val = ev
        if k in s.dcnt:
            val = s.dcnt[k]
        elif k == eng and eng == 'pe':
            return
        if s.seen[eng].get(k, 0) >= val:
            return
        s.seen[eng][k] = val
        waits.append((k, val))

    def op(s, eng, fn, R=(), W=(), dma=None):
        waits = []
        if eng == 'sp':
            for ev in s.gbar:
                s._need(eng, ev, waits)
        for b in R:
            s._need(eng, b.w, waits)
        for b in W:
            s._need(eng, b.w, waits)
            for k, v in b.r.items():
                s._need(eng, (k, v), waits)
        if dma is None:
            s.cnt[eng] += 1
            ev = (eng, s.cnt[eng])
        else:
            s.dcnt[dma] = s.dcnt.get(dma, 0) + 16
            ev = (dma, s.dcnt[dma])
        for b in R:
            if b.r.get(ev[0], 0) < ev[1]:
                b.r[ev[0]] = ev[1]
        for b in W:
            b.w = ev
            b.r = {}
        s.ops[eng].append((waits, fn, ev[0], dma is not None))

    def emit(s):
        nc = s.nc
        sems = {}
        for k in list(s.ENG) + list(s.dcnt):
            sems[k] = s.es.enter_context(nc.semaphore('s_' + k))

        def run(eng, e):
            for waits, fn, k, isd in s.ops[eng]:
                for (wk, wv) in waits:
                    e.wait_ge(sems[wk], wv)
                fn(e).then_inc(sems[k], 16 if isd else 1)
            if eng == 'sp':
                for k, v in s.dcnt.items():
                    e.wait_ge(sems[k], v)
                for k in ('pe', 'act', 'dve', 'pool'):
                    if s.cnt[k]:
                        e.wait_ge(sems[k], s.cnt[k])

        with nc.Block() as blk:
            blk.tensor(lambda e: run('pe', e))
            blk.scalar(lambda e: run('act', e))
            blk.vector(lambda e: run('dve', e))
            blk.gpsimd(lambda e: run('pool', e))
            blk.sync(lambda e: run('sp', e))


class _Stop(Exception):
    pass


DBG_STOP = [0]
import os
SKIP = os.environ.get('SKIP', '')


def build(SEQ, DEPTH, PAST):
    stage = [0]

    def CK():
        stage[0] += 1
        if DBG_STOP[0] and stage[0] >= DBG_STOP[0]:
            raise _Stop()
    nc = bass.Bass("TRN2", target_bir_lowering=False)
    LT = NM + SEQ
    NG = SEQ // 512
    NBP = 1 + SEQ // 128
    NBS = PAST // 128 + 1
    es = ExitStack()
    C = Ctx(nc, es)

    def IN(name, shape, dt=F32):
        return C.dram(name, shape, dt, kind="ExternalInput")

    def OUT(name, shape):
        return C.dram(name, shape, F32, kind="ExternalOutput")

    xT_in = IN("xT_in", [D, LT])
    xsT_in = IN("xsT_in", [2, D, DS])
    ck_in = IN("ck", [DEPTH, 2, PAST, 512])
    cv_in = IN("cv", [DEPTH, 2, PAST, 512])
    s_lh = IN("s_lh", [DEPTH, 2, 128, 2])
    s_lc = IN("s_lc", [DEPTH, 2, 128, 6])
    s_gla = IN("s_gla", [DEPTH, 2, 128, 64])
    s_fc = IN("s_fc", [DEPTH, 2, 128, 44])
    par_in = IN("par", [DEPTH, 128, NP])
    gfin_in = IN("gfin", [128, 8])
    wg_in = IN("wgate", [DEPTH, 4, 128, 128])
    w2_in = IN("w2", [DEPTH, 16, 128])
    w_in = IN("w_in", [DEPTH, D, INW])
    w_out = IN("w_out", [DEPTH, D, D])
    w_up = IN("w_up", [DEPTH, D, FF])
    w_gt = IN("w_gt", [DEPTH, D, FF])
    w_dn = IN("w_dn", [DEPTH, FF, D])
    c_ident = IN("c_ident", [128, 128])
    c_masks = IN("c_masks", [4, 128, 512])
    c_tri = IN("c_tri", [128, 512])
    c_bones = IN("c_bones", [128, 128])
    c_hmask = IN("c_hmask", [128, 4])
    c_bmask = IN("c_bmask", [128, 256])

    yT = OUT("yT", [D, LT])
    ysT = OUT("ysT", [2, D, DS])
    k_p = OUT("k_p", [DEPTH, LT, 512])
    v_p = OUT("v_p", [DEPTH, LT, 512])
    lh_p = OUT("lh_p", [DEPTH, 128, 2])
    lc_p = OUT("lc_p", [DEPTH, 128, 6])
    gla_p = OUT("gla_p", [DEPTH, 128, 64])
    fc_p = OUT("fc_p", [DEPTH, 128, 44])
    k_s = OUT("k_s", [DEPTH, 2, DS, 512])
    v_s = OUT("v_s", [DEPTH, 2, DS, 512])
    lh_s = OUT("lh_s", [DEPTH, 2, 128, 2])
    lc_s = OUT("lc_s", [DEPTH, 2, 128, 6])
    gla_s = OUT("gla_s", [DEPTH, 2, 128, 64])
    fc_s = OUT("fc_s", [DEPTH, 2, 128, 44])

    wb_in = C.dram("wb_in", [DEPTH, D, INW], BF16)
    wb_out = C.dram("wb_out", [DEPTH, D, D], BF16)
    wb_up = C.dram("wb_up", [DEPTH, D, FF], BF16)
    wb_gt = C.dram("wb_gt", [DEPTH, D, FF], BF16)
    wb_dn = C.dram("wb_dn", [DEPTH, FF, D], BF16)
    xsc = C.dram("xsc", [D, LT], F32)
    xssc = C.dram("xssc", [2, D, DS], F32)
    kT_p = C.dram("kT_p", [4, 128, NBP * 128], BF16)
    vv_p = C.dram("vv_p", [NBP * 128, 512], BF16)
    kT_s = [C.dram(f"kT_s{i}", [4, 128, NBS * 128], BF16) for i in range(2)]
    vv_s = [C.dram(f"vv_s{i}", [NBS * 128, 512], BF16) for i in range(2)]

    x = [C.sb('x', [128, 512], F32) for _ in range(8)]
    hn = [C.sb('hn', [128, 512], BF16) for _ in range(8)]
    pan = [C.sb('pan', [128, 11 * 512], BF16) for _ in range(3)]
    act = [C.sb('act', [128, 512], BF16) for _ in range(11)]
    mix = [C.sb('mix', [128, 512], BF16) for _ in range(8)]
    qp = [[C.sb('qp', [128, 512], BF16) for c in range(2)] for h in range(4)]
    tmp = [C.sb('tmp', [128, 516], F32) for _ in range(14)]
    lx = [C.sb('lx', [128, 515], F32) for _ in range(2)]
    lg = [C.sb('lg', [128, 512], F32) for _ in range(2)]
    gq = C.sb('gq', [128, 512], F32)
    gk = C.sb('gk', [128, 512], F32)
    gog = [C.sb('gog', [128, 512], F32) for _ in range(2)]
    glr = C.sb('glr', [16, 512], F32)
    kts = C.sb('kts', [128, 4 * 512], BF16)
    kout = [C.sb('kout', [128, 512], F32) for _ in range(2)]
    vout = [C.sb('vout', [128, 512], F32) for _ in range(2)]
    vbf = [C.sb('vbf', [128, 512], BF16) for _ in range(2)]
    gv = C.sb('gv', [128, 4 * 512], BF16)
    kld = [C.sb('kld', [128, 512], BF16) for _ in range(3)]
    vld = [C.sb('vld', [128, 1024], BF16) for _ in range(3)]
    pt = [C.sb('pt', [128, 512], BF16) for _ in range(4)]
    sqb = [C.sb('sqb', [128, 512], BF16) for _ in range(2)]
    bt = [C.sb('bt', [128, 512], BF16) for _ in range(4)]
    ones_b = C.sb('ones', [128, 128], BF16)
    ident = C.sb('ident', [128, 128], F32)
    identb = C.sb('identb', [128, 128], BF16)
    masks = [C.sb('mask', [128, 512], BF16) for _ in range(4)]
    tri = C.sb('tri', [128, 512], F32)
    bones = C.sb('bones', [128, 128], BF16)
    hmask = C.sb('hmask', [128, 4], F32)
    bmask = C.sb('bmask', [128, 256], F32)
    par = C.sb('par', [128, NP], F32)
    gfin = C.sb('gfin', [128, 8], F32)
    wg = [C.sb('wg', [128, 128], F32) for _ in range(4)]
    w2 = C.sb('w2', [16, 128], F32)
    hprev = C.sb('hprev', [128, 2], F32)
    S_all = C.sb('S_all', [128, 64], F32)
    fst = C.sb('fst', [128, 44], F32)
    spad = [C.sb('spad', [128, 128], BF16) for _ in range(4)]
    misc = C.sb('misc', [128, 16], F32)
    ps = [C.ps('ps', [128, 512]) for _ in range(8)]
    pbi = [0]

    def PB():
        pbi[0] += 1
        return ps[pbi[0] % 8]

    evi = [0]

    def evac(out_ap, in_ap, R, W, eng=None):
        if eng is None:
            evi[0] += 1
            eng = 'act' if evi[0] % 2 else 'dve'
        if eng == 'act':
            C.op('act', lambda e: e.activation(out=out_ap, in_=in_ap, func=AF.Copy), R=R, W=W)
        else:
            C.op('dve', lambda e: e.tensor_copy(out_ap, in_ap), R=R, W=W)

    def MM(ob, out_ap, lhsT, rhs, st, sp_, R):
        C.op('pe', lambda e: e.matmul(out_ap, lhsT=lhsT, rhs=rhs, start=st, stop=sp_), R=R, W=[ob])

    def DMA(out_ap, in_ap, R, W, sem, q='sp', slow=False):
        if slow:
            C.op(q, lambda e: e.dma_start(out=out_ap, in_=in_ap, allow_slow_non_contiguous=True), R=R, W=W, dma=sem)
        else:
            C.op(q, lambda e: e.dma_start(out=out_ap, in_=in_ap), R=R, W=W, dma=sem)

    def TT(eng, out, in0, in1, op, R, W):
        C.op(eng, lambda e: e.tensor_tensor(out=out, in0=in0, in1=in1, op=op), R=R, W=W)

    def TS(eng, out, in0, s1, s2, op0, op1, R, W):
        if s2 is None:
            C.op(eng, lambda e: e.tensor_scalar(out, in0, s1, None, op0), R=R, W=W)
        else:
            C.op(eng, lambda e: e.tensor_scalar(out, in0, s1, s2, op0, op1), R=R, W=W)

    def STT(out, in0, sc, in1, op0, op1, R, W):
        C.op('dve', lambda e: e.scalar_tensor_tensor(out=out, in0=in0, scalar=sc, in1=in1, op0=op0, op1=op1), R=R, W=W)

    def ACT(out, in_, func, R, W, bias=0.0, scale=1.0):
        C.op('act', lambda e: e.activation(out=out, in_=in_, func=func, bias=bias, scale=scale), R=R, W=W)

    C.op('pool', lambda e: e.memset(ones_b[:], 1.0), W=[ones_b])
    DMA(ident[:], c_ident[:, :], [], [ident], 'cst')
    DMA(tri[:], c_tri[:, :], [], [tri], 'cst')
    DMA(hmask[:], c_hmask[:, :], [], [hmask], 'cst')
    DMA(bmask[:], c_bmask[:, :], [], [bmask], 'cst')
    DMA(gfin[:], gfin_in[:, :], [], [gfin], 'cst')
    for i in range(4):
        DMA(tmp[i][:, 0:512], c_masks.t[i, :, :], [], [tmp[i]], 'cst')
        C.op('dve', lambda e, i=i: e.tensor_copy(masks[i][:], tmp[i][:, 0:512]), R=[tmp[i]], W=[masks[i]])
    DMA(tmp[4][:, 0:128], c_bones[:, :], [], [tmp[4]], 'cst')
    C.op('dve', lambda e: e.tensor_copy(bones[:], tmp[4][:, 0:128]), R=[tmp[4]], W=[bones])
    C.op('dve', lambda e: e.tensor_copy(identb[:], ident[:]), R=[ident], W=[identb])
    for h in range(4):
        C.op('pool', lambda e, h=h: e.memset(qp[h][0][:], 0.0), W=[qp[h][0]])
        C.op('pool', lambda e, h=h: e.memset(qp[h][1][:], 0.0), W=[qp[h][1]])
        C.op('pool', lambda e, h=h: e.memset(spad[h][:], 0.0), W=[spad[h]])
    C.op('pool', lambda e: e.memset(gv[:], 0.0), W=[gv])

    cvi = [0]

    def convert(src, dst, l, rows, cols):
        for r0 in range(0, rows, 128):
            for c0 in range(0, cols, 512):
                cw = min(512, cols - c0)
                i = cvi[0] % 4
                cvi[0] += 1
                t, b = tmp[i], pt[i]
                DMA(t[:, :cw], src.t[l, r0:r0 + 128, c0:c0 + cw], [], [t], 'cvl%d' % i)
                eng = ('dve', 'pool', 'act', 'pool')[i]
                if eng == 'act':
                    C.op('act', lambda e, t=t, b=b, cw=cw: e.activation(out=b[:, :cw], in_=t[:, :cw], func=AF.Copy), R=[t], W=[b])
                else:
                    C.op(eng, lambda e, t=t, b=b, cw=cw: e.tensor_copy(b[:, :cw], t[:, :cw]), R=[t], W=[b])
                DMA(dst.t[l, r0:r0 + 128, c0:c0 + cw], b[:, :cw], [b], [], 'cvs%d' % i, q='pool')

    class V:
        def __init__(s, buf, l):
            s.buf = buf
            s.t = buf.t[l]

    def rmsnorm_stats(T):
        acc = ps[7]
        for kc in range(8):
            sq = sqb[kc % 2]
            if kc % 2:
                ACT(sq[:, :T], x[kc][:, :T], AF.Square, [x[kc]], [sq])
            else:
                TT('pool', sq[:, :T], x[kc][:, :T], x[kc][:, :T], ALU.mult, [x[kc]], [sq])
            MM(acc, acc[:, :T], ones_b[:, :], sq[:, :T], kc == 0, kc == 7, [ones_b, sq])
        ACT(tmp[13][:, :T], acc[:, :T], AF.Ln, [acc], [tmp[13]], bias=EPS, scale=1.0 / D)
        ACT(tmp[13][:, :T], tmp[13][:, :T], AF.Exp, [tmp[13]], [tmp[13]], scale=-0.5)

    def norm_to_hn(T, gbuf, g0):
        rmsnorm_stats(T)
        for kc in range(8):
            STT(hn[kc][:, :T], x[kc][:, :T], gbuf[:, g0 + kc:g0 + kc + 1], tmp[13][:, :T], ALU.mult, ALU.mult,
                [x[kc], gbuf, tmp[13]], [hn[kc]])

    pani = [0]

    def load_panel(wv, r0, nk, c0, ncols):
        i = pani[0] % int(os.environ.get('NPAN', '3'))
        pani[0] += 1
        b = pan[i]
        view = b.t[:, 0:nk * ncols].rearrange("p (k n) -> p k n", k=nk)
        src = wv.t[r0:r0 + nk * 128, c0:c0 + ncols].rearrange("(k p) n -> p k n", p=128)
        DMA(view, src, [wv.buf], [b], 'pan%d' % i)
        return b, view

    def sigmoid_inplace(t_ap, tb, eng='dve'):
        TS('pool', t_ap, t_ap, 1.0, None, ALU.add, None, [tb], [tb])
        C.op('dve', lambda e: e.reciprocal(t_ap, t_ap), R=[tb], W=[tb])

    def group_layer(l, T, xsrc, xdst, ydst, kT_d, vv_d, blk0, kpos_rows, chunks, kout_d, vout_d, last):
        NT = (T + 127) // 128
        CT = min(128, T)
        Win, Wout, Wup, Wgt, Wdn = V(wb_in, l), V(wb_out, l), V(wb_up, l), V(wb_gt, l), V(wb_dn, l)
        lam_init = 0.8 - 0.6 * math.exp(-0.3 * l)
        for kc in range(8):
            DMA(x[kc][:, :T], xsrc[1][kc * 128:(kc + 1) * 128, :], [xsrc[0]], [x[kc]], 'xl')
        norm_to_hn(T, par, O_G1)
        CK()
        b, vw = load_panel(Win, 0, 8, 0, 512)
        for h in range(4):
            p_ = PB()
            for kc in range(8):
                MM(p_, p_[:, :T], vw[:, kc, h * 128:(h + 1) * 128], hn[kc][:, :T], kc == 0, kc == 7, [b, hn[kc]])
            en_ = 'act' if h % 2 else 'dve'
            evac(qp[h][0][0:64, :T], p_[0:64, :T], [p_], [qp[h][0]], en_)
            evac(qp[h][1][64:128, :T], p_[64:128, :T], [p_], [qp[h][1]], en_)
        CK()
        b, vw = load_panel(Win, 0, 8, 512, 512)
        for h in range(4):
            p_ = PB()
            for kc in range(8):
                MM(p_, p_[:, :T], vw[:, kc, h * 128:(h + 1) * 128], hn[kc][:, :T], kc == 0, kc == 7, [b, hn[kc]])
            evac(kts[:, h * 512:h * 512 + T], p_[:, :T], [p_], [kts])
        CK()
        DMA(kT_d.t[:, :, blk0 * 128:blk0 * 128 + T].rearrange("h p k -> p h k"),
            kts[:, :].rearrange("p (h k) -> p h k", h=4)[:, :, :T], [kts], [kT_d], 'kts', q='pool')
        CK()
        for tt in range(NT):
            tw = min(128, T - tt * 128)
            p_ = PB()
            for kc in range(8):
                MM(p_, p_[:tw, :], hn[kc][:, tt * 128:tt * 128 + tw], vw[:, kc, :], kc == 0, kc == 7, [b, hn[kc]])
            ko = kout[tt % 2]
            evac(ko[:tw, :], p_[:tw, :], [p_], [ko])
            DMA(kout_d[1][tt * 128:tt * 128 + tw, :], ko[:tw, :], [ko], [kout_d[0]], 'ko%d' % (tt % 2), q='pool')
        CK()
        b, vw = load_panel(Win, 0, 8, 1024, 512)
        for tt in range(NT):
            tw = min(128, T - tt * 128)
            p_ = PB()
            for kc in range(8):
                MM(p_, p_[:tw, :], hn[kc][:, tt * 128:tt * 128 + tw], vw[:, kc, :], kc == 0, kc == 7, [b, hn[kc]])
            vo, vb_ = vout[tt % 2], vbf[tt % 2]
            C.op('act', lambda e, vo=vo, p_=p_, tw=tw: e.activation(out=vo[:tw, :], in_=p_[:tw, :], func=AF.Copy), R=[p_], W=[vo])
            C.op('pool', lambda e, vb_=vb_, vo=vo, tw=tw: e.tensor_copy(vb_[:tw, :], vo[:tw, :]), R=[vo], W=[vb_])
            if 'A' not in SKIP: DMA(vout_d[1][tt * 128:tt * 128 + tw, :], vo[:tw, :], [vo], [vout_d[0]], 'vo%d' % (tt % 2), q='pool')
            if 'B' not in SKIP: DMA(vv_d.t[(blk0 + tt) * 128:(blk0 + tt) * 128 + tw, :], vb_[:tw, :], [vb_], [vv_d], 'vb', q='pool')
        CK()
        b, vw = load_panel(Win, 0, 8, 1536, 512)
        for j in range(4):
            p_ = PB()
            for kc in range(8):
                MM(p_, p_[:, :T], vw[:, kc, j * 128:(j + 1) * 128], hn[kc][:, :T], kc == 0, kc == 7, [b, hn[kc]])
            if j < 2:
                evac(lx[j][:, 3:3 + T], p_[:, :T], [p_], [lx[j]])
            else:
                evac(lg[j - 2][:, :T], p_[:, :T], [p_], [lg[j - 2]])
        CK()
        b, vw = load_panel(Win, 0, 8, 2048, 512)
        for j in range(2):
            p_ = PB()
            for kc in range(8):
                MM(p_, p_[:, :T], vw[:, kc, j * 128:(j + 1) * 128], hn[kc][:, :T], kc == 0, kc == 7, [b, hn[kc]])
            dst = gq if j == 0 else gk
            evac(dst[:, :T], p_[:, :T], [p_], [dst])
        for tt in range(NT):
            tw = min(128, T - tt * 128)
            p_ = PB()
            for kc in range(8):
                MM(p_, p_[:tw, 0:256], hn[kc][:, tt * 128:tt * 128 + tw], vw[:, kc, 256:512], kc == 0, kc == 7, [b, hn[kc]])
            for h in range(4):
                o0 = (tt * 4 + h) * 128 + (h % 2) * 64
                evac(gv[:tw, o0:o0 + 64], p_[:tw, h * 64:(h + 1) * 64], [p_], [gv], 'act' if tt % 2 else 'dve')
        CK()
        b, vw = load_panel(Win, 0, 8, 2560, 272)
        p_ = PB()
        for kc in range(8):
            MM(p_, p_[0:16, :T], vw[:, kc, 0:16], hn[kc][:, :T], kc == 0, kc == 7, [b, hn[kc]])
        evac(glr[0:16, :T], p_[0:16, :T], [p_], [glr])
        for j in range(2):
            p_ = PB()
            for kc in range(8):
                MM(p_, p_[:, :T], vw[:, kc, 16 + j * 128:16 + (j + 1) * 128], hn[kc][:, :T], kc == 0, kc == 7, [b, hn[kc]])
            evac(gog[j][:, :T], p_[:, :T], [p_], [gog[j]])

        CK()
        nlam = misc[:, 2:3]
        gsub = misc[:, 3:4]
        li = [0]
        for h in range(4):
            O = [ps[4], ps[5]]
            L = [ps[6], ps[7]]
            first = True
            nb_tot = sum((8 if c[3] else c[1]) for c in chunks)
            bi = 0
            for (cb, nblk, nk_last, masked) in chunks:
                i = li[0] % 3
                li[0] += 1
                kt_, vt_ = kld[i], vld[i]
                nkeys = (nblk - 1) * 128 + nk_last
                DMA(kt_[:, :nkeys], kT_d.t[h, :, cb * 128:cb * 128 + nkeys], [kT_d], [kt_], 'kl%d' % i)
                if masked:
                    DMA(vt_[0:64, :].rearrange("p (b e) -> p b e", b=8),
                        vv_d.t[cb * 128:(cb + 4) * 128, h * 128:(h + 1) * 128].rearrange("(b p) e -> p b e", p=64), [vv_d], [vt_], 'vl%d' % i)
                    blocks = [(hb * 64, 64, hb * 128, hb * 64) for hb in range(8)]
                elif nblk > 1:
                    DMA(vt_[:, :nblk * 128].rearrange("p (b e) -> p b e", b=nblk),
                        vv_d.t[cb * 128:(cb + nblk) * 128, h * 128:(h + 1) * 128].rearrange("(b p) e -> p b e", p=128), [vv_d], [vt_], 'vl%d' % i)
                    blocks = [(bb * 128, 128, bb * 128, 0) for bb in range(nblk)]
                else:
                    DMA(vt_[:nk_last, 0:128], vv_d.t[cb * 128:cb * 128 + nk_last, h * 128:(h + 1) * 128], [vv_d], [vt_], 'vl%d' % i)
                    blocks = [(0, nk_last, 0, 0)]
                for (ko_, nk, vo_, q0) in blocks:
                    bi += 1
                    lastb = bi == nb_tot
                    for c in range(2):
                        S = ps[(bi % 2) * 2 + c]
                        P = pt[(bi % 2) * 2 + c]
                        MM(S, S[:nk, q0:T], kt_[:, ko_:ko_ + nk], qp[h][c][:, q0:T], True, True, [kt_, qp[h][c]])
                        ACT(P[:nk, q0:T], S[:nk, q0:T], AF.Exp, [S], [P], scale=0.125)
                        MM(O[c], O[c][:, q0:T], vt_[:nk, vo_:vo_ + 128], P[:nk, q0:T], first, lastb, [vt_, P])
                        MM(L[c], L[c][:, q0:T], ones_b[:nk, :], P[:nk, q0:T], first, lastb, [ones_b, P])
                    first = False
            r1, t1, r2, t2 = tmp[0], tmp[1], tmp[2], tmp[3]
            C.op('dve', lambda e: e.reciprocal(r1[:, :T], L[0][:, :T]), R=[L[0]], W=[r1])
            TT('dve', t1[:, :T], O[0][:, :T], r1[:, :T], ALU.mult, [O[0], r1], [t1])
            C.op('dve', lambda e: e.reciprocal(r2[:, :T], L[1][:, :T]), R=[L[1]], W=[r2])
            TT('dve', t2[:, :T], O[1][:, :T], r2[:, :T], ALU.mult, [O[1], r2], [t2])
            STT(t1[:, :T], t2[:, :T], nlam, t1[:, :T], ALU.mult, ALU.add, [t2, t1, misc], [t1])
            TT('pool', sqb[0][:, :T], t1[:, :T], t1[:, :T], ALU.mult, [t1], [sqb[0]])
            MM(ps[0], ps[0][:, :T], ones_b[:, :], sqb[0][:, :T], True, True, [ones_b, sqb[0]])
            ACT(r1[:, :T], ps[0][:, :T], AF.Ln, [ps[0]], [r1], bias=EPS, scale=1.0 / 128)
            ACT(r1[:, :T], r1[:, :T], AF.Exp, [r1], [r1], scale=-0.5)
            STT(mix[h][:, :T], t1[:, :T], gsub, r1[:, :T], ALU.mult, ALU.mult, [t1, misc, r1], [mix[h]])

        CK()
        for j in range(2):
            xc, ea, ei, a_, om, u_, h_, g1 = tmp[0], tmp[1], tmp[2], tmp[3], tmp[4], tmp[5], tmp[6], tmp[7]
            lxj = lx[j]
            w0 = O_LCW + 4 * j
            TS('dve', xc[:, :T], lxj[:, 0:T], par[:, w0:w0 + 1], par[:, O_LCB + j:O_LCB + j + 1], ALU.mult, ALU.add, [lxj, par], [xc])
            for t_ in range(1, 4):
                STT(xc[:, :T], lxj[:, t_:t_ + T], par[:, w0 + t_:w0 + t_ + 1], xc[:, :T], ALU.mult, ALU.add, [lxj, par, xc], [xc])
            C.op('pool', lambda e, lxj=lxj: e.tensor_copy(misc[:, 8:11], lxj[:, T:T + 3]), R=[lxj], W=[misc])
            C.op('pool', lambda e, lxj=lxj: e.tensor_copy(lxj[:, 0:3], misc[:, 8:11]), R=[misc], W=[lxj])
            pa, pi_ = PB(), PB()
            MM(pa, pa[:, :T], wg[j][:, :], xc[:, :T], True, True, [wg[j], xc])
            MM(pi_, pi_[:, :T], wg[2 + j][:, :], xc[:, :T], True, True, [wg[2 + j], xc])
            ACT(ea[:, :T], pa[:, :T], AF.Exp, [pa, misc], [ea], bias=misc[:, 4 + j:5 + j], scale=-1.0)
            ACT(ei[:, :T], pi_[:, :T], AF.Exp, [pi_, misc], [ei], bias=misc[:, 6 + j:7 + j], scale=-1.0)
            sigmoid_inplace(ea[:, :T], ea)
            sigmoid_inplace(ei[:, :T], ei)
            ACT(a_[:, :T], ea[:, :T], AF.Exp, [ea, misc], [a_], scale=misc[:, 11 + j:12 + j])
            TT('pool', om[:, :T], a_[:, :T], a_[:, :T], ALU.mult, [a_], [om])
            ACT(om[:, :T], om[:, :T], AF.Ln, [om], [om], bias=1.0, scale=-1.0)
            ACT(om[:, :T], om[:, :T], AF.Exp, [om], [om], scale=0.5)
            TT('dve', u_[:, :T], ei[:, :T], xc[:, :T], ALU.mult, [ei, xc], [u_])
            TT('dve', u_[:, :T], u_[:, :T], om[:, :T], ALU.mult, [u_, om], [u_])
            C.op('dve', lambda e, j=j: e.tensor_tensor_scan(h_[:, :T], a_[:, :T], u_[:, :T], hprev[:, j:j + 1], ALU.mult, ALU.add),
                 R=[a_, u_, hprev], W=[h_])
            C.op('dve', lambda e, j=j: e.tensor_copy(hprev[:, j:j + 1], h_[:, T - 1:T]), R=[h_], W=[hprev])
            lgj = lg[j]
            TT('pool', g1[:, :T], lgj[:, :T], lgj[:, :T], ALU.mult, [lgj], [g1])
            TS('pool', g1[:, :T], g1[:, :T], 0.044715, 1.0, ALU.mult, ALU.add, [g1], [g1])
            TT('pool', g1[:, :T], g1[:, :T], lgj[:, :T], ALU.mult, [g1, lgj], [g1])
            ACT(g1[:, :T], g1[:, :T], AF.Exp, [g1], [g1], scale=-1.5957691216057308)
            sigmoid_inplace(g1[:, :T], g1)
            TT('dve', g1[:, :T], g1[:, :T], lgj[:, :T], ALU.mult, [g1, lgj], [g1])
            TT('dve', mix[4 + j][:, :T], g1[:, :T], h_[:, :T], ALU.mult, [g1, h_], [mix[4 + j]])

        CK()
        gg = tmp[8]
        p_ = PB()
        MM(p_, p_[:, :T], w2[0:16, :], glr[0:16, :T], True, True, [w2, glr])
        ACT(gg[:, :T], p_[:, :T], AF.Exp, [p_, misc], [gg], bias=misc[:, 13:14], scale=-1.0)
        ACT(gg[:, :T], gg[:, :T], AF.Ln, [gg], [gg], bias=1.0, scale=1.0)
        TS('dve', gg[:, :T], gg[:, :T], -1.0 / 16.0, None, ALU.mult, None, [gg], [gg])
        bc, eb, enb, eh = tmp[9], tmp[10], tmp[11], tmp[12]
        qs, ks, kh, aT = bt[0], bt[1], bt[2], bt[3]
        oT = [tmp[0], tmp[1]]
        C.op('pool', lambda e: e.memset(tmp[13][:, 0:128], 1.0), W=[tmp[13]])
        for ci in range((T + CT - 1) // CT):
            c0 = ci * CT
            sl = slice(c0, c0 + CT)
            C.op('dve', lambda e, sl=sl: e.tensor_tensor_scan(bc[:, :CT], tmp[13][:, :CT], gg[:, sl], 0.0, ALU.mult, ALU.add),
                 R=[tmp[13], gg], W=[bc])
            ACT(eb[:, :CT], bc[:, :CT], AF.Exp, [bc], [eb])
            ACT(enb[:, :CT], bc[:, :CT], AF.Exp, [bc], [enb], scale=-1.0)
            ACT(eh[:, :CT], bc[:, :CT], AF.Exp, [bc], [eh], bias=bc[:, CT - 1:CT], scale=-1.0)
            ACT(misc[:, 14:15], bc[:, CT - 1:CT], AF.Exp, [bc], [misc])
            STT(qs[:, :CT], gq[:, sl], 32.0 ** -0.5, eb[:, :CT], ALU.mult, ALU.mult, [gq, eb], [qs])
            TT('dve', ks[:, :CT], gk[:, sl], enb[:, :CT], ALU.mult, [gk, enb], [ks])
            TT('dve', kh[:, :CT], gk[:, sl], eh[:, :CT], ALU.mult, [gk, eh], [kh])
            pa = PB()
            for h in range(4):
                kz = pt[h]
                TS('pool', kz[:, :CT], ks[:, :CT], hmask[:, h:h + 1], None, ALU.mult, None, [ks, hmask], [kz])
                MM(pa, pa[:CT, h * 128:h * 128 + CT], kz[:, :CT], qs[:, :CT], True, True, [kz, qs])
            for h in range(4):
                TT('dve', aT[:CT, h * 128:h * 128 + CT], pa[:CT, h * 128:h * 128 + CT], tri[:CT, 0:CT], ALU.mult, [pa, tri], [aT])
            ptk = PB()
            C.op('pe', lambda e, ptk=ptk: e.transpose(ptk.t.bitcast(BF16)[:CT, 0:128], kh[:, :CT], identb[:, :]),
                 R=[kh, identb], W=[ptk])
            kht = sqb[1]
            evac(kht[:CT, 0:128], ptk.t.bitcast(BF16)[:CT, 0:128], [ptk], [kht])
            for pr in range(2):
                po = PB()
                for k2 in range(2):
                    h = pr * 2 + k2
                    o0 = (ci * 4 + h) * 128
                    MM(po, po[:, :CT], gv[:CT, o0:o0 + 128], aT[:CT, h * 128:h * 128 + CT], k2 == 0, False, [gv, aT])
                    MM(po, po[:, :CT], spad[h][:, :], qs[:, :CT], False, k2 == 1, [spad[h], qs])
                evac(oT[pr][:, sl], po[:, :CT], [po], [oT[pr]])
            pd = PB()
            gvu = sqb[0]
            for h in range(4):
                o0 = (ci * 4 + h) * 128 + (h % 2) * 64
                C.op('pool', lambda e, h=h, o0=o0: e.tensor_copy(gvu[:CT, h * 64:(h + 1) * 64], gv[:CT, o0:o0 + 64]), R=[gv], W=[gvu])
            MM(pd, pd[:, 0:256], kht[:CT, 0:128], gvu[:CT, 0:256], True, True, [kht, gvu])
            dS = tmp[2]
            TT('dve', dS[:, 0:256], pd[:, 0:256], bmask[:, :], ALU.mult, [pd, bmask], [dS])
            C.op('dve', lambda e: e.tensor_reduce(out=dS[:, 256:320], in_=dS[:, 0:256].rearrange("p (h e) -> p e h", h=4), axis=AX.X, op=ALU.add),
                 R=[dS], W=[dS])
            STT(S_all[:, :], S_all[:, :], misc[:, 14:15], dS[:, 256:320], ALU.mult, ALU.add, [S_all, misc, dS], [S_all])
            for h in range(4):
                TS('pool', spad[h][:, (h % 2) * 64:(h % 2) * 64 + 64], S_all[:, :], hmask[:, h:h + 1], None, ALU.mult, None,
                   [S_all, hmask], [spad[h]])
        for pr in range(2):
            o_ = oT[pr]
            TT('pool', sqb[0][:, :T], o_[:, :T], o_[:, :T], ALU.mult, [o_], [sqb[0]])
            p_ = PB()
            MM(p_, p_[:, :T], bones[:, :], sqb[0][:, :T], True, True, [bones, sqb[0]])
            rs = tmp[2]
            ACT(rs[:, :T], p_[:, :T], AF.Ln, [p_], [rs], bias=EPS, scale=1.0 / 64)
            ACT(rs[:, :T], rs[:, :T], AF.Exp, [rs], [rs], scale=-0.5)
            STT(o_[:, :T], o_[:, :T], par[:, O_GNG:O_GNG + 1], rs[:, :T], ALU.mult, ALU.mult, [o_, par, rs], [o_])
            sg = tmp[3]
            ACT(sg[:, :T], gog[pr][:, :T], AF.Exp, [gog[pr]], [sg], scale=-1.0)
            sigmoid_inplace(sg[:, :T], sg)
            TT('dve', sg[:, :T], sg[:, :T], gog[pr][:, :T], ALU.mult, [sg, gog[pr]], [sg])
            TT('dve', mix[6 + pr][:, :T], sg[:, :T], o_[:, :T], ALU.mult, [sg, o_], [mix[6 + pr]])

        CK()
        for half in range(2):
            b, vw = load_panel(Wout, 0, 8, half * 512, 512)
            for f4 in range(4):
                f = half * 4 + f4
                p_ = PB()
                for kc in range(8):
                    MM(p_, p_[:, :T], vw[:, kc, f4 * 128:(f4 + 1) * 128], mix[kc][:, :T], kc == 0, kc == 7, [b, mix[kc]])
                TT('dve', x[f][:, :T], p_[:, :T], x[f][:, :T], ALU.add, [p_, x[f]], [x[f]])
        CK()
        norm_to_hn(T, par, O_G2)
        for half in range(2):
            fl = 0
            for c0, ncols in ((0, 512), (512, 512), (1024, 384)):
                cc = half * 1408 + c0
                bu, vu = load_panel(Wup, 0, 8, cc, ncols)
                bg, vg = load_panel(Wgt, 0, 8, cc, ncols)
                for f4 in range(ncols // 128):
                    f = half * 11 + fl
                    pu, pg = PB(), PB()
                    for kc in range(8):
                        MM(pu, pu[:, :T], vu[:, kc, f4 * 128:(f4 + 1) * 128], hn[kc][:, :T], kc == 0, kc == 7, [bu, hn[kc]])
                    for kc in range(8):
                        MM(pg, pg[:, :T], vg[:, kc, f4 * 128:(f4 + 1) * 128], hn[kc][:, :T], kc == 0, kc == 7, [bg, hn[kc]])
                    up = tmp[4 + (fl % 2) * 4]
                    uc = tmp[5 + (fl % 2) * 4]
                    g1 = tmp[6 + (fl % 2) * 4]
                    C.op('act', lambda e, up=up, pu=pu: e.activation(out=up[:, 2:2 + T], in_=pu[:, :T], func=AF.Copy), R=[pu], W=[up])
                    C.op('pool', lambda e, up=up, f=f: e.tensor_copy(up[:, 0:2], fst[:, 2 * f:2 * f + 2]), R=[fst], W=[up])
                    C.op('pool', lambda e, up=up, f=f: e.tensor_copy(fst[:, 2 * f:2 * f + 2], up[:, T:T + 2]), R=[up], W=[fst])
                    wc = O_FCW + 3 * f
                    TS('dve', uc[:, :T], up[:, 0:T], par[:, wc:wc + 1], par[:, O_FCB + f:O_FCB + f + 1], ALU.mult, ALU.add, [up, par], [uc])
                    STT(uc[:, :T], up[:, 1:1 + T], par[:, wc + 1:wc + 2], uc[:, :T], ALU.mult, ALU.add, [up, par, uc], [uc])
                    STT(uc[:, :T], up[:, 2:2 + T], par[:, wc + 2:wc + 3], uc[:, :T], ALU.mult, ALU.add, [up, par, uc], [uc])
                    TT('pool', g1[:, :T], uc[:, :T], uc[:, :T], ALU.mult, [uc], [g1])
                    TS('pool', g1[:, :T], g1[:, :T], 0.044715, 1.0, ALU.mult, ALU.add, [g1], [g1])
                    TT('pool', g1[:, :T], g1[:, :T], uc[:, :T], ALU.mult, [g1, uc], [g1])
                    ACT(g1[:, :T], g1[:, :T], AF.Exp, [g1], [g1], scale=-1.5957691216057308)
                    sigmoid_inplace(g1[:, :T], g1)
                    TT('pool', g1[:, :T], g1[:, :T], uc[:, :T], ALU.mult, [g1, uc], [g1])
                    TT('dve', act[fl][:, :T], pg[:, :T], g1[:, :T], ALU.mult, [pg, g1], [act[fl]])
                    fl += 1
            for hc in range(2):
                b, vw = load_panel(Wdn, half * 1408, 11, hc * 512, 512)
                for f4 in range(4):
                    f = hc * 4 + f4
                    p_ = PB()
                    for kc in range(11):
                        MM(p_, p_[:, :T], vw[:, kc, f4 * 128:(f4 + 1) * 128], act[kc][:, :T], kc == 0, kc == 10, [b, act[kc]])
                    TT('dve', x[f][:, :T], p_[:, :T], x[f][:, :T], ALU.add, [p_, x[f]], [x[f]])
        CK()
        if not last:
            for kc in range(8):
                DMA(xdst[1][kc * 128:(kc + 1) * 128, :], x[kc][:, :T], [x[kc]], [xdst[0]], 'xs', q='pool')
        else:
            rmsnorm_stats(T)
            for kc in range(8):
                yo = tmp[kc % 4]
                STT(yo[:, :T], x[kc][:, :T], gfin[:, kc:kc + 1], tmp[13][:, :T], ALU.mult, ALU.mult, [x[kc], gfin, tmp[13]], [yo])
                DMA(ydst[1][kc * 128:(kc + 1) * 128, :], yo[:, :T], [yo], [ydst[0]], 'yo%d' % (kc % 4), q='pool')

    def layer_setup(l):
        DMA(par[:], par_in.t[l, :, :], [], [par], 'par')
        for i in range(4):
            DMA(wg[i][:], wg_in.t[l, i, :, :], [], [wg[i]], 'par')
        DMA(w2[:], w2_in.t[l, :, :], [], [w2], 'par')
        lam_init = 0.8 - 0.6 * math.exp(-0.3 * l)
        al = par[:, O_AL:O_AL + 256]
        TT('dve', tmp[0][:, 0:64], par[:, O_AL:O_AL + 64], par[:, O_AL + 64:O_AL + 128], ALU.mult, [par], [tmp[0]])
        TT('dve', tmp[0][:, 64:128], par[:, O_AL + 128:O_AL + 192], par[:, O_AL + 192:O_AL + 256], ALU.mult, [par], [tmp[0]])
        C.op('dve', lambda e: e.tensor_reduce(out=misc[:, 0:1], in_=tmp[0][:, 0:64], axis=AX.X, op=ALU.add), R=[tmp[0]], W=[misc])
        C.op('dve', lambda e: e.tensor_reduce(out=misc[:, 1:2], in_=tmp[0][:, 64:128], axis=AX.X, op=ALU.add), R=[tmp[0]], W=[misc])
        ACT(misc[:, 0:2], misc[:, 0:2], AF.Exp, [misc], [misc])
        TT('dve', misc[:, 2:3], misc[:, 1:2], misc[:, 0:1], ALU.subtract, [misc], [misc])
        TS('dve', misc[:, 2:3], misc[:, 2:3], -lam_init, None, ALU.add, None, [misc], [misc])
        TS('dve', misc[:, 3:4], par[:, O_SUBG:O_SUBG + 1], 1.0 - lam_init, None, ALU.mult, None, [par], [misc])
        TS('dve', misc[:, 4:6], par[:, O_BA:O_BA + 2], -1.0, None, ALU.mult, None, [par], [misc])
        TS('dve', misc[:, 6:8], par[:, O_BX:O_BX + 2], -1.0, None, ALU.mult, None, [par], [misc])
        TS('dve', misc[:, 13:14], par[:, O_GGB:O_GGB + 1], -1.0, None, ALU.mult, None, [par], [misc])
        ACT(misc[:, 11:13], par[:, O_LAM:O_LAM + 2], AF.Exp, [par], [misc], scale=-1.0)
        ACT(misc[:, 11:13], misc[:, 11:13], AF.Ln, [misc], [misc], bias=1.0)
        TS('dve', misc[:, 11:13], misc[:, 11:13], -8.0, None, ALU.mult, None, [misc], [misc])

    def state_zero():
        C.op('pool', lambda e: e.memset(hprev[:], 0.0), W=[hprev])
        C.op('pool', lambda e: e.memset(S_all[:], 0.0), W=[S_all])
        C.op('pool', lambda e: e.memset(fst[:], 0.0), W=[fst])
        for j in range(2):
            C.op('pool', lambda e, j=j: e.memset(lx[j][:, 0:3], 0.0), W=[lx[j]])
        for h in range(4):
            C.op('pool', lambda e, h=h: e.memset(spad[h][:], 0.0), W=[spad[h]])

    def state_load(l, s):
        DMA(hprev[:], s_lh.t[l, s, :, :], [], [hprev], 'st')
        DMA(S_all[:], s_gla.t[l, s, :, :], [], [S_all], 'st')
        DMA(fst[:], s_fc.t[l, s, :, :], [], [fst], 'st')
        DMA(tmp[0][:, 0:6], s_lc.t[l, s, :, :], [], [tmp[0]], 'st')
        for j in range(2):
            C.op('pool', lambda e, j=j: e.tensor_copy(lx[j][:, 0:3], tmp[0][:, 3 * j:3 * j + 3]), R=[tmp[0]], W=[lx[j]])
        for h in range(4):
            TS('pool', spad[h][:, (h % 2) * 64:(h % 2) * 64 + 64], S_all[:, :], hmask[:, h:h + 1], None, ALU.mult, None,
               [S_all, hmask], [spad[h]])

    def state_store(lh_ap, lc_ap, gla_ap, fc_ap, obuf):
        DMA(lh_ap, hprev[:], [hprev], [obuf], 'sto', q='pool')
        DMA(gla_ap, S_all[:], [S_all], [obuf], 'sto', q='pool')
        DMA(fc_ap, fst[:], [fst], [obuf], 'sto', q='pool')
        for j in range(2):
            DMA(lc_ap[:, 3 * j:3 * j + 3], lx[j][:, 0:3], [lx[j]], [obuf], 'sto', q='pool')

    try:
        for l in range(DEPTH):
            for (src, dst, rows, cols) in ((w_in, wb_in, D, INW), (w_out, wb_out, D, D), (w_up, wb_up, D, FF),
                                           (w_gt, wb_gt, D, FF), (w_dn, wb_dn, FF, D)):
                convert(src, dst, l, rows, cols)
        C.gbar = [('cvs%d' % i, C.dcnt['cvs%d' % i]) for i in range(4)]

        obuf = C.dram("dummy_track", [1, 1], F32)
        for l in range(DEPTH):
            layer_setup(l)
            state_zero()
            last = l == DEPTH - 1
            xs_src = xT_in if l == 0 else xsc
            group_layer(l, NM, (xs_src, xs_src.t[:, 0:NM]), (xsc, xsc.t[:, 0:NM]), (yT, yT.t[:, 0:NM]), kT_p, vv_p, 0, 0,
                        [(0, 1, NM, False)], (k_p, k_p.t[l, 0:NM, :]), (v_p, v_p.t[l, 0:NM, :]), last)
            for g in range(NG):
                t0 = NM + 512 * g
                chunks = [(0, 1, NM, False)] + [(1 + 4 * j, 4, 128, False) for j in range(g)] + [(1 + 4 * g, 4, 128, True)]
                group_layer(l, 512, (xs_src, xs_src.t[:, t0:t0 + 512]), (xsc, xsc.t[:, t0:t0 + 512]), (yT, yT.t[:, t0:t0 + 512]),
                            kT_p, vv_p, 1 + 4 * g, 0, chunks, (k_p, k_p.t[l, t0:t0 + 512, :]), (v_p, v_p.t[l, t0:t0 + 512, :]), last)
            state_store(lh_p.t[l, :, :], lc_p.t[l, :, :], gla_p.t[l, :, :], fc_p.t[l, :, :], obuf)
        for l in range(DEPTH):
            layer_setup(l)
            last = l == DEPTH - 1
            for s in range(2):
                for bb in range(PAST // 128):
                    i = bb % 2
                    kc_, vc_ = tmp[4 + i], tmp[6 + i]
                    DMA(kc_[:, 0:512], ck_in.t[l, s, bb * 128:(bb + 1) * 128, :], [], [kc_], 'ckl%d' % i)
                    DMA(vc_[:, 0:512], cv_in.t[l, s, bb * 128:(bb + 1) * 128, :], [], [vc_], 'cvl2%d' % i)
                    p_ = PB()
                    for h in range(4):
                        C.op('pe', lambda e, p_=p_, kc_=kc_, h=h: e.transpose(p_[:, h * 128:(h + 1) * 128], kc_[:, h * 128:(h + 1) * 128], ident[:, :]),
                             R=[kc_, ident], W=[p_])
                    kb_, vb_ = bt[i], bt[2 + i]
                    evac(kb_[:, :], p_[:, :], [p_], [kb_])
                    C.op('pool', lambda e, vb_=vb_, vc_=vc_: e.tensor_copy(vb_[:, :], vc_[:, 0:512]), R=[vc_], W=[vb_])
                    DMA(kT_s[s].t[:, :, bb * 128:(bb + 1) * 128].rearrange("h p k -> p h k"),
                        kb_[:, :].rearrange("p (h k) -> p h k", h=4), [kb_], [kT_s[s]], 'cks', q='pool')
                    DMA(vv_s[s].t[bb * 128:(bb + 1) * 128, :], vb_[:, :], [vb_], [vv_s[s]], 'cvs2', q='pool')
                state_load(l, s)
                nb = PAST // 128
                chunks = [(4 * j, 4, 128, False) for j in range(nb // 4)] + [(nb, 1, DS, False)]
                xs_src = xsT_in if l == 0 else xssc
                group_layer(l, DS, (xs_src, xs_src.t[s, :, :]), (xssc, xssc.t[s, :, :]), (ysT, ysT.t[s, :, :]), kT_s[s], vv_s[s], nb, 0,
                            chunks, (k_s, k_s.t[l, s, :, :]), (v_s, v_s.t[l, s, :, :]), last)
                state_store(lh_s.t[l, s, :, :], lc_s.t[l, s, :, :], gla_s.t[l, s, :, :], fc_s.t[l, s, :, :], obuf)

    except _Stop:
        pass
    print('SBUF remaining', nc.sbuf_bytes_remaining, flush=True)
    C.emit()
    es.close()
    return nc


_NC_CACHE = {}


def _consts():
    p = np.arange(128)
    q = np.arange(512)
    masks = np.zeros((4, 128, 512), np.float32)
    for kb in range(4):
        masks[kb] = (((kb * 128 + p) // 64)[:, None] <= (q // 64)[None, :]).astype(np.float32)
    tri = np.zeros((128, 512), np.float32)
    tri[:, :128] = (p[:, None] <= p[None, :]).astype(np.float32)
    bones = ((p // 64)[:, None] == (p // 64)[None, :]).astype(np.float32)
    hmask = ((p // 32)[:, None] == np.arange(4)[None, :]).astype(np.float32)
    bmask = ((p // 32)[:, None] == (np.arange(256) // 64)[None, :]).astype(np.float32)
    return dict(c_ident=np.eye(128, dtype=np.float32), c_masks=masks, c_tri=tri, c_bones=bones, c_hmask=hmask, c_bmask=bmask)


def kernel(**inp):
    f = lambda k: np.ascontiguousarray(np.asarray(inp[k], dtype=np.float32))
    x_prompt, x_sample = f('x_prompt'), f('x_sample')
    B, SEQ, _ = x_prompt.shape
    NSB = x_sample.shape[0]
    ck, cv = f('cache_attn_k'), f('cache_attn_v')
    DEPTH, _, PAST = ck.shape[0], ck.shape[1], ck.shape[2]
    LT = NM + SEQ
    key = (SEQ, DEPTH, PAST)
    if key not in _NC_CACHE:
        _NC_CACHE[key] = build(SEQ, DEPTH, PAST)
    nc = _NC_CACHE[key]
    meta = f('meta_tokens')
    par = np.zeros((DEPTH, 128, NP), np.float32)
    colmaj = lambda a, n: a.reshape(DEPTH, n, 128).transpose(0, 2, 1)
    par[:, :, O_G1:O_G1 + 8] = colmaj(f('norm_mix_g'), 8)
    par[:, :, O_G2:O_G2 + 8] = colmaj(f('norm_ffn_g'), 8)
    lcw = f('lru_conv_w')
    par[:, :, O_LCW:O_LCW + 8] = lcw.reshape(DEPTH, 4, 2, 128).transpose(0, 3, 2, 1).reshape(DEPTH, 128, 8)
    par[:, :, O_LCB:O_LCB + 2] = colmaj(f('lru_conv_b'), 2)
    par[:, :, O_BA:O_BA + 2] = colmaj(f('lru_gate_a_b'), 2)
    par[:, :, O_BX:O_BX + 2] = colmaj(f('lru_gate_x_b'), 2)
    par[:, :, O_LAM:O_LAM + 2] = colmaj(f('lru_log_lambda'), 2)
    par[:, :, O_GGB] = f('gla_gate_b')
    par[:, :, O_GNG] = np.tile(f('gla_norm_g'), (1, 2))
    par[:, :, O_SUBG] = f('attn_subln_g')
    fcw = f('ffn_conv_w')
    par[:, :, O_FCW:O_FCW + 66] = fcw.reshape(DEPTH, 3, 22, 128).transpose(0, 3, 2, 1).reshape(DEPTH, 128, 66)
    par[:, :, O_FCB:O_FCB + 22] = colmaj(f('ffn_conv_b'), 22)
    par[:, :, O_AL:O_AL + 256] = f('attn_lambda').reshape(DEPTH, 1, 256)
    gfin = f('norm_final_g').reshape(8, 128).T.copy()
    wgate = np.zeros((DEPTH, 4, 128, 128), np.float32)
    for gi, nm in enumerate(('lru_gate_a_w', 'lru_gate_x_w')):
        w = f(nm)
        for n in range(8):
            j, r = n // 4, (n % 4) * 32
            wgate[:, gi * 2 + j, r:r + 32, r:r + 32] = w[:, n]
    shared = dict(par=par, gfin=gfin, wgate=wgate, w2=f('gla_gate_w2'), w_in=f('w_in'), w_out=f('w_out'), w_up=f('ffn_w_up'),
                  w_gt=f('ffn_w_gate'), w_dn=f('ffn_w_down'))
    shared.update(_consts())
    slh, slc, sgl, sfc = f('state_lru_h'), f('state_lru_conv'), f('state_gla'), f('state_ffn_conv')
    in_maps = []
    for c in range(8):
        bp = c % B
        ss = [(2 * c) % NSB, (2 * c + 1) % NSB]
        m = dict(shared)
        m['xT_in'] = np.ascontiguousarray(np.concatenate([meta, x_prompt[bp]], 0).T)
        m['xsT_in'] = np.ascontiguousarray(x_sample[ss].transpose(0, 2, 1))
        m['ck'] = np.ascontiguousarray(ck[:, ss].reshape(DEPTH, 2, PAST, 512))
        m['cv'] = np.ascontiguousarray(cv[:, ss].reshape(DEPTH, 2, PAST, 512))
        m['s_lh'] = np.ascontiguousarray(slh[:, ss].reshape(DEPTH, 2, 2, 128).transpose(0, 1, 3, 2))
        m['s_lc'] = np.ascontiguousarray(slc[:, ss].reshape(DEPTH, 2, 3, 2, 128).transpose(0, 1, 4, 3, 2).reshape(DEPTH, 2, 128, 6))
        m['s_gla'] = np.ascontiguousarray(sgl[:, ss].reshape(DEPTH, 2, 128, 64))
        m['s_fc'] = np.ascontiguousarray(sfc[:, ss].reshape(DEPTH, 2, 2, 22, 128).transpose(0, 1, 4, 3, 2).reshape(DEPTH, 2, 128, 44))
        in_maps.append(m)
    res = run_bass_kernel_spmd(nc, in_maps, core_ids=list(range(8)))
    R = res.results
    _NC_CACHE['last_results'] = R
    y_prompt = np.stack([R[b]['yT'].T[NM:] for b in range(B)])
    k_p = np.stack([R[b]['k_p'] for b in range(B)], 1).reshape(DEPTH, B, LT, 4, 128)
    v_p = np.stack([R[b]['v_p'] for b in range(B)], 1).reshape(DEPTH, B, LT, 4, 128)
    lh_p = np.stack([R[b]['lh_p'].transpose(0, 2, 1).reshape(DEPTH, 256) for b in range(B)], 1)
    lc_p = np.stack([R[b]['lc_p'].reshape(DEPTH, 128, 2, 3).transpose(0, 3, 2, 1).reshape(DEPTH, 3, 256) for b in range(B)], 1)
    gla_p = np.stack([R[b]['gla_p'].reshape(DEPTH, 4, 32, 64) for b in range(B)], 1)
    fc_p = np.stack([R[b]['fc_p'].reshape(DEPTH, 128, 22, 2).transpose(0, 3, 2, 1).reshape(DEPTH, 2, FF) for b in range(B)], 1)
    ncs = NSB // 2
    cat = lambda k, ax: np.concatenate([R[c][k] for c in range(ncs)], ax)
    y_sample = cat('ysT', 0).transpose(0, 2, 1)
    k_s = cat('k_s', 1).reshape(DEPTH, NSB, DS, 4, 128)
    v_s = cat('v_s', 1).reshape(DEPTH, NSB, DS, 4, 128)
    lh_s = cat('lh_s', 1).transpose(0, 1, 3, 2).reshape(DEPTH, NSB, 256)
    lc_s = cat('lc_s', 1).reshape(DEPTH, NSB, 128, 2, 3).transpose(0, 1, 4, 3, 2).reshape(DEPTH, NSB, 3, 256)
    gla_s = cat('gla_s', 1).reshape(DEPTH, NSB, 4, 32, 64)
    fc_s = cat('fc_s', 1).reshape(DEPTH, NSB, 128, 22, 2).transpose(0, 1, 4, 3, 2).reshape(DEPTH, NSB, 2, FF)
    outs = (y_prompt, y_sample, k_p, v_p, lh_p, lc_p, gla_p, fc_p, k_s, v_s, lh_s, lc_s, gla_s, fc_s)
    return tuple(np.ascontiguousarray(o, dtype=np.float32) for o in outs)
```

```python
import math
import numpy as np
from contextlib import ExitStack
import concourse.bass as bass
import concourse.mybir as mybir
from concourse.bass_utils import run_bass_kernel_spmd

F32 = mybir.dt.float32
BF16 = mybir.dt.bfloat16
ALU = mybir.AluOpType
AF = mybir.ActivationFunctionType
AX = mybir.AxisListType

D = 1024
NM = 16
DS = 64
EPS = 1e-6
INW = 2832
FF = 2816
NP = 379
O_G1, O_G2, O_LCW, O_LCB, O_BA, O_BX, O_LAM, O_GGB, O_GNG, O_SUBG, O_FCW, O_FCB, O_AL = 0, 8, 16, 24, 26, 28, 30, 32, 33, 34, 35, 101, 123


class Buf:
    __slots__ = ('t', 'w', 'r', 'name')

    def __init__(s, t, name):
        s.t = t
        s.w = None
        s.r = {}
        s.name = name

    def __getitem__(s, k):
        return s.t[k]


class Ctx:
    ENG = ('pe', 'act', 'dve', 'pool', 'sp')

    def __init__(s, nc, es):
        s.nc = nc
        s.es = es
        s.ops = {e: [] for e in s.ENG}
        s.cnt = {e: 0 for e in s.ENG}
        s.seen = {e: {} for e in s.ENG}
        s.dcnt = {}
        s.nb = 0
        s.gbar = []

    def sb(s, name, shape, dt):
        s.nb += 1
        return Buf(s.es.enter_context(s.nc.sbuf_tensor(f"{name}_{s.nb}", list(shape), dt)), name)

    def ps(s, name, shape, dt=F32):
        s.nb += 1
        return Buf(s.es.enter_context(s.nc.psum_tensor(f"{name}_{s.nb}", list(shape), dt)), name)

    def dram(s, name, shape, dt, kind="Internal"):
        return Buf(s.nc.dram_tensor(name, list(shape), dt, kind=kind), name)

    def _need(s, eng, ev, waits):
        if ev is None:
            return
        k, val = ev
        if k in s.dcnt:
            val = s.dcnt[k]
        elif k == eng and eng == 'pe':
            return
        if s.seen[eng].get(k, 0) >= val:
            return
        s.seen[eng][k] = val
        waits.append((k, val))

    def op(s, eng, fn, R=(), W=(), dma=None):
        waits = []
        if eng == 'sp':
            for ev in s.gbar:
                s._need(eng, ev, waits)
        for b in R:
            s._need(eng, b.w, waits)
        for b in W:
            s._need(eng, b.w, waits)
            for k, v in b.r.items():
                s._need(eng, (k, v), waits)
        if dma is None:
            s.cnt[eng] += 1
            ev = (eng, s.cnt[eng])
        else:
            s.dcnt[dma] = s.dcnt.get(dma, 0) + 16
            ev = (dma, s.dcnt[dma])
        for b in R:
            if b.r.get(ev[0], 0) < ev[1]:
                b.r[ev[0]] = ev[1]
        for b in W:
            b.w = ev
            b.r = {}
        s.ops[eng].append((waits, fn, ev[0], dma is not None))

    def emit(s):
        nc = s.nc
        sems = {}
        for k in list(s.ENG) + list(s.dcnt):
            sems[k] = s.es.enter_context(nc.semaphore('s_' + k))

        def run(eng, e):
            for waits, fn, k, isd in s.ops[eng]:
                for (wk, wv) in waits:
                    e.wait_ge(sems[wk], wv)
                fn(e).then_inc(sems[k], 16 if isd else 1)
            if eng == 'sp':
                for k, v in s.dcnt.items():
                    e.wait_ge(sems[k], v)
                for k in ('pe', 'act', 'dve', 'pool'):
                    if s.cnt[k]:
                        e.wait_ge(sems[k], s.cnt[k])

        with nc.Block() as blk:
            blk.tensor(lambda e: run('pe', e))
            blk.scalar(lambda e: run('act', e))
            blk.vector(lambda e: run('dve', e))
            blk.gpsimd(lambda e: run('pool', e))
            blk.sync(lambda e: run('sp', e))


class _Stop(Exception):
    pass


DBG_STOP = [0]
import os
SKIP = os.environ.get('SKIP', '')


def build(SEQ, DEPTH, PAST):
    stage = [0]

    def CK():
        stage[0] += 1
        if DBG_STOP[0] and stage[0] >= DBG_STOP[0]:
            raise _Stop()
    nc = bass.Bass("TRN2", target_bir_lowering=False)
    LT = NM + SEQ
    NG = SEQ // 512
    NBP = 1 + SEQ // 128
    NBS = PAST // 128 + 1
    es = ExitStack()
    C = Ctx(nc, es)

    def IN(name, shape, dt=F32):
        return C.dram(name, shape, dt, kind="ExternalInput")

    def OUT(name, shape):
        return C.dram(name, shape, F32, kind="ExternalOutput")

    xT_in = IN("xT_in", [D, LT])
    xsT_in = IN("xsT_in", [2, D, DS])
    ck_in = IN("ck", [DEPTH, 2, PAST, 512])
    cv_in = IN("cv", [DEPTH, 2, PAST, 512])
    s_lh = IN("s_lh", [DEPTH, 2, 128, 2])
    s_lc = IN("s_lc", [DEPTH, 2, 128, 6])
    s_gla = IN("s_gla", [DEPTH, 2, 128, 64])
    s_fc = IN("s_fc", [DEPTH, 2, 128, 44])
    par_in = IN("par", [DEPTH, 128, NP])
    gfin_in = IN("gfin", [128, 8])
    wg_in = IN("wgate", [DEPTH, 4, 128, 128])
    w2_in = IN("w2", [DEPTH, 16, 128])
    w_in = IN("w_in", [DEPTH, D, INW])
    w_out = IN("w_out", [DEPTH, D, D])
    w_up = IN("w_up", [DEPTH, D, FF])
    w_gt = IN("w_gt", [DEPTH, D, FF])
    w_dn = IN("w_dn", [DEPTH, FF, D])
    c_ident = IN("c_ident", [128, 128])
    c_masks = IN("c_masks", [4, 128, 512])
    c_tri = IN("c_tri", [128, 512])
    c_bones = IN("c_bones", [128, 128])
    c_hmask = IN("c_hmask", [128, 4])
    c_bmask = IN("c_bmask", [128, 256])

    yT = OUT("yT", [D, LT])
    ysT = OUT("ysT", [2, D, DS])
    k_p = OUT("k_p", [DEPTH, LT, 512])
    v_p = OUT("v_p", [DEPTH, LT, 512])
    lh_p = OUT("lh_p", [DEPTH, 128, 2])
    lc_p = OUT("lc_p", [DEPTH, 128, 6])
    gla_p = OUT("gla_p", [DEPTH, 128, 64])
    fc_p = OUT("fc_p", [DEPTH, 128, 44])
    k_s = OUT("k_s", [DEPTH, 2, DS, 512])
    v_s = OUT("v_s", [DEPTH, 2, DS, 512])
    lh_s = OUT("lh_s", [DEPTH, 2, 128, 2])
    lc_s = OUT("lc_s", [DEPTH, 2, 128, 6])
    gla_s = OUT("gla_s", [DEPTH, 2, 128, 64])
    fc_s = OUT("fc_s", [DEPTH, 2, 128, 44])

    wb_in = C.dram("wb_in", [DEPTH, D, INW], BF16)
    wb_out = C.dram("wb_out", [DEPTH, D, D], BF16)
    wb_up = C.dram("wb_up", [DEPTH, D, FF], BF16)
    wb_gt = C.dram("wb_gt", [DEPTH, D, FF], BF16)
    wb_dn = C.dram("wb_dn", [DEPTH, FF, D], BF16)
    xsc = C.dram("xsc", [D, LT], F32)
    xssc = C.dram("xssc", [2, D, DS], F32)
    kT_p = C.dram("kT_p", [4, 128, NBP * 128], BF16)
    vv_p = C.dram("vv_p", [NBP * 128, 512], BF16)
    kT_s = [C.dram(f"kT_s{i}", [4, 128, NBS * 128], BF16) for i in range(2)]
    vv_s = [C.dram(f"vv_s{i}", [NBS * 128, 512], BF16) for i in range(2)]

    x = [C.sb('x', [128, 512], F32) for _ in range(8)]
    hn = [C.sb('hn', [128, 512], BF16) for _ in range(8)]
    pan = [C.sb('pan', [128, 11 * 512], BF16) for _ in range(3)]
    act = [C.sb('act', [128, 512], BF16) for _ in range(11)]
    mix = [C.sb('mix', [128, 512], BF16) for _ in range(8)]
    qp = [[C.sb('qp', [128, 512], BF16) for c in range(2)] for h in range(4)]
    tmp = [C.sb('tmp', [128, 516], F32) for _ in range(14)]
    lx = [C.sb('lx', [128, 515], F32) for _ in range(2)]
    lg = [C.sb('lg', [128, 512], F32) for _ in range(2)]
    gq = C.sb('gq', [128, 512], F32)
    gk = C.sb('gk', [128, 512], F32)
    gog = [C.sb('gog', [128, 512], F32) for _ in range(2)]
    glr = C.sb('glr', [16, 512], F32)
    kts = C.sb('kts', [128, 4 * 512], BF16)
    kout = [C.sb('kout', [128, 512], F32) for _ in range(2)]
    vout = [C.sb('vout', [128, 512], F32) for _ in range(2)]
    vbf = [C.sb('vbf', [128, 512], BF16) for _ in range(2)]
    gv = C.sb('gv', [128, 4 * 512], BF16)
    kld = [C.sb('kld', [128, 512], BF16) for _ in range(3)]
    vld = [C.sb('vld', [128, 1024], BF16) for _ in range(3)]
    pt = [C.sb('pt', [128, 512], BF16) for _ in range(4)]
    sqb = [C.sb('sqb', [128, 512], BF16) for _ in range(2)]
    bt = [C.sb('bt', [128, 512], BF16) for _ in range(4)]
    ones_b = C.sb('ones', [128, 128], BF16)
    ones_f = C.sb('onesf', [128, 128], F32)
    ident = C.sb('ident', [128, 128], F32)
    identb = C.sb('identb', [128, 128], BF16)
    masks = [C.sb('mask', [128, 512], BF16) for _ in range(4)]
    tri = C.sb('tri', [128, 512], F32)
    bones = C.sb('bones', [128, 128], BF16)
    hmask = C.sb('hmask', [128, 4], F32)
    bmask = C.sb('bmask', [128, 256], F32)
    par = C.sb('par', [128, NP], F32)
    gfin = C.sb('gfin', [128, 8], F32)
    wg = [C.sb('wg', [128, 128], F32) for _ in range(4)]
    w2 = C.sb('w2', [16, 128], F32)
    hprev = C.sb('hprev', [128, 2], F32)
    S_all = C.sb('S_all', [128, 64], F32)
    fst = C.sb('fst', [128, 44], F32)
    spad = [C.sb('spad', [128, 128], BF16) for _ in range(4)]
    misc = C.sb('misc', [128, 16], F32)
    ps = [C.ps('ps', [128, 512]) for _ in range(8)]
    pbi = [0]

    def PB():
        pbi[0] += 1
        return ps[pbi[0] % 8]

    evi = [0]

    def evac(out_ap, in_ap, R, W, eng=None):
        if eng is None:
            evi[0] += 1
            eng = 'act' if evi[0] % 2 else 'dve'
        if eng == 'act':
            C.op('act', lambda e: e.activation(out=out_ap, in_=in_ap, func=AF.Copy), R=R, W=W)
        else:
            C.op('dve', lambda e: e.tensor_copy(out_ap, in_ap), R=R, W=W)

    def MM(ob, out_ap, lhsT, rhs, st, sp_, R):
        C.op('pe', lambda e: e.matmul(out_ap, lhsT=lhsT, rhs=rhs, start=st, stop=sp_), R=R, W=[ob])

    def DMA(out_ap, in_ap, R, W, sem, q='sp', slow=False):
        if slow:
            C.op(q, lambda e: e.dma_start(out=out_ap, in_=in_ap, allow_slow_non_contiguous=True), R=R, W=W, dma=sem)
        else:
            C.op(q, lambda e: e.dma_start(out=out_ap, in_=in_ap), R=R, W=W, dma=sem)

    def TT(eng, out, in0, in1, op, R, W):
        C.op(eng, lambda e: e.tensor_tensor(out=out, in0=in0, in1=in1, op=op), R=R, W=W)

    def TS(eng, out, in0, s1, s2, op0, op1, R, W):
        if s2 is None:
            C.op(eng, lambda e: e.tensor_scalar(out, in0, s1, None, op0), R=R, W=W)
        else:
            C.op(eng, lambda e: e.tensor_scalar(out, in0, s1, s2, op0, op1), R=R, W=W)

    def STT(out, in0, sc, in1, op0, op1, R, W):
        C.op('dve', lambda e: e.scalar_tensor_tensor(out=out, in0=in0, scalar=sc, in1=in1, op0=op0, op1=op1), R=R, W=W)

    def ACT(out, in_, func, R, W, bias=0.0, scale=1.0):
        C.op('act', lambda e: e.activation(out=out, in_=in_, func=func, bias=bias, scale=scale), R=R, W=W)

    C.op('pool', lambda e: e.memset(ones_b[:], 1.0), W=[ones_b])
    C.op('pool', lambda e: e.memset(ones_f[:], 1.0), W=[ones_f])
    DMA(ident[:], c_ident[:, :], [], [ident], 'cst')
    DMA(tri[:], c_tri[:, :], [], [tri], 'cst')
    DMA(hmask[:], c_hmask[:, :], [], [hmask], 'cst')
    DMA(bmask[:], c_bmask[:, :], [], [bmask], 'cst')
    DMA(gfin[:], gfin_in[:, :], [], [gfin], 'cst')
    for i in range(4):
        DMA(tmp[i][:, 0:512], c_masks.t[i, :, :], [], [tmp[i]], 'cst')
        C.op('dve', lambda e, i=i: e.tensor_copy(masks[i][:], tmp[i][:, 0:512]), R=[tmp[i]], W=[masks[i]])
    DMA(tmp[4][:, 0:128], c_bones[:, :], [], [tmp[4]], 'cst')
    C.op('dve', lambda e: e.tensor_copy(bones[:], tmp[4][:, 0:128]), R=[tmp[4]], W=[bones])
    C.op('dve', lambda e: e.tensor_copy(identb[:], ident[:]), R=[ident], W=[identb])
    for h in range(4):
        C.op('pool', lambda e, h=h: e.memset(qp[h][0][:], 0.0), W=[qp[h][0]])
        C.op('pool', lambda e, h=h: e.memset(qp[h][1][:], 0.0), W=[qp[h][1]])
        C.op('pool', lambda e, h=h: e.memset(spad[h][:], 0.0), W=[spad[h]])
    C.op('pool', lambda e: e.memset(gv[:], 0.0), W=[gv])

    cvi = [0]

    def convert(src, dst, l, rows, cols):
        for r0 in range(0, rows, 128):
            for c0 in range(0, cols, 512):
                cw = min(512, cols - c0)
                i = cvi[0] % 4
                cvi[0] += 1
                t, b = tmp[i], pt[i]
                DMA(t[:, :cw], src.t[l, r0:r0 + 128, c0:c0 + cw], [], [t], 'cvl%d' % i)
                eng = ('dve', 'pool', 'act', 'pool')[i]
                if eng == 'act':
                    C.op('act', lambda e, t=t, b=b, cw=cw: e.activation(out=b[:, :cw], in_=t[:, :cw], func=AF.Copy), R=[t], W=[b])
                else:
                    C.op(eng, lambda e, t=t, b=b, cw=cw: e.tensor_copy(b[:, :cw], t[:, :cw]), R=[t], W=[b])
                DMA(dst.t[l, r0:r0 + 128, c0:c0 + cw], b[:, :cw], [b], [], 'cvs%d' % i, q='pool')

    class V:
        def __init__(s, buf, l):
            s.buf = buf
            s.t = buf.t[l]

    def rmsnorm_stats(T):
        acc = ps[7]
        for kc in range(8):
            sq = sqb[kc % 2]
            if kc % 2:
                ACT(sq[:, :T], x[kc][:, :T], AF.Square, [x[kc]], [sq])
            else:
                TT('pool', sq[:, :T], x[kc][:, :T], x[kc][:, :T], ALU.mult, [x[kc]], [sq])
            MM(acc, acc[:, :T], ones_b[:, :], sq[:, :T], kc == 0, kc == 7, [ones_b, sq])
        ACT(tmp[13][:, :T], acc[:, :T], AF.Ln, [acc], [tmp[13]], bias=EPS, scale=1.0 / D)
        ACT(tmp[13][:, :T], tmp[13][:, :T], AF.Exp, [tmp[13]], [tmp[13]], scale=-0.5)

    def norm_to_hn(T, gbuf, g0):
        rmsnorm_stats(T)
        for kc in range(8):
            STT(hn[kc][:, :T], x[kc][:, :T], gbuf[:, g0 + kc:g0 + kc + 1], tmp[13][:, :T], ALU.mult, ALU.mult,
                [x[kc], gbuf, tmp[13]], [hn[kc]])

    pani = [0]

    def load_panel(wv, r0, nk, c0, ncols):
        i = pani[0] % int(os.environ.get('NPAN', '3'))
        pani[0] += 1
        b = pan[i]
        view = b.t[:, 0:nk * ncols].rearrange("p (k n) -> p k n", k=nk)
        src = wv.t[r0:r0 + nk * 128, c0:c0 + ncols].rearrange("(k p) n -> p k n", p=128)
        DMA(view, src, [wv.buf], [b], 'pan%d' % i)
        return b, view

    def sigmoid_inplace(t_ap, tb, eng='dve'):
        TS('pool', t_ap, t_ap, 1.0, None, ALU.add, None, [tb], [tb])
        C.op('dve', lambda e: e.reciprocal(t_ap, t_ap), R=[tb], W=[tb])

    def group_layer(l, T, xsrc, xdst, ydst, kT_d, vv_d, blk0, kpos_rows, chunks, kout_d, vout_d, last):
        NT = (T + 127) // 128
        CT = min(128, T)
        Win, Wout, Wup, Wgt, Wdn = V(wb_in, l), V(wb_out, l), V(wb_up, l), V(wb_gt, l), V(wb_dn, l)
        lam_init = 0.8 - 0.6 * math.exp(-0.3 * l)
        for kc in range(8):
            DMA(x[kc][:, :T], xsrc[1][kc * 128:(kc + 1) * 128, :], [xsrc[0]], [x[kc]], 'xl')
        norm_to_hn(T, par, O_G1)
        CK()
        b, vw = load_panel(Win, 0, 8, 0, 512)
        for h in range(4):
            p_ = PB()
            for kc in range(8):
                MM(p_, p_[:, :T], vw[:, kc, h * 128:(h + 1) * 128], hn[kc][:, :T], kc == 0, kc == 7, [b, hn[kc]])
            en_ = 'act' if h % 2 else 'dve'
            evac(qp[h][0][0:64, :T], p_[0:64, :T], [p_], [qp[h][0]], en_)
            evac(qp[h][1][64:128, :T], p_[64:128, :T], [p_], [qp[h][1]], en_)
        CK()
        b, vw = load_panel(Win, 0, 8, 512, 512)
        for h in range(4):
            p_ = PB()
            for kc in range(8):
                MM(p_, p_[:, :T], vw[:, kc, h * 128:(h + 1) * 128], hn[kc][:, :T], kc == 0, kc == 7, [b, hn[kc]])
            evac(kts[:, h * 512:h * 512 + T], p_[:, :T], [p_], [kts])
        CK()
        DMA(kT_d.t[:, :, blk0 * 128:blk0 * 128 + T].rearrange("h p k -> p h k"),
            kts[:, :].rearrange("p (h k) -> p h k", h=4)[:, :, :T], [kts], [kT_d], 'kts', q='pool')
        CK()
        for tt in range(NT):
            tw = min(128, T - tt * 128)
            p_ = PB()
            for kc in range(8):
                MM(p_, p_[:tw, :], hn[kc][:, tt * 128:tt * 128 + tw], vw[:, kc, :], kc == 0, kc == 7, [b, hn[kc]])
            ko = kout[tt % 2]
            evac(ko[:tw, :], p_[:tw, :], [p_], [ko])
            DMA(kout_d[1][tt * 128:tt * 128 + tw, :], ko[:tw, :], [ko], [kout_d[0]], 'ko%d' % (tt % 2), q='pool')
        CK()
        b, vw = load_panel(Win, 0, 8, 1024, 512)
        for tt in range(NT):
            tw = min(128, T - tt * 128)
            p_ = PB()
            for kc in range(8):
                MM(p_, p_[:tw, :], hn[kc][:, tt * 128:tt * 128 + tw], vw[:, kc, :], kc == 0, kc == 7, [b, hn[kc]])
            vo, vb_ = vout[tt % 2], vbf[tt % 2]
            C.op('act', lambda e, vo=vo, p_=p_, tw=tw: e.activation(out=vo[:tw, :], in_=p_[:tw, :], func=AF.Copy), R=[p_], W=[vo])
            C.op('pool', lambda e, vb_=vb_, vo=vo, tw=tw: e.tensor_copy(vb_[:tw, :], vo[:tw, :]), R=[vo], W=[vb_])
            if 'A' not in SKIP: DMA(vout_d[1][tt * 128:tt * 128 + tw, :], vo[:tw, :], [vo], [vout_d[0]], 'vo%d' % (tt % 2), q='pool')
            if 'B' not in SKIP: DMA(vv_d.t[(blk0 + tt) * 128:(blk0 + tt) * 128 + tw, :], vb_[:tw, :], [vb_], [vv_d], 'vb', q='pool')
        CK()
        b, vw = load_panel(Win, 0, 8, 1536, 512)
        for j in range(4):
            p_ = PB()
            for kc in range(8):
                MM(p_, p_[:, :T], vw[:, kc, j * 128:(j + 1) * 128], hn[kc][:, :T], kc == 0, kc == 7, [b, hn[kc]])
            if j < 2:
                evac(lx[j][:, 3:3 + T], p_[:, :T], [p_], [lx[j]])
            else:
                evac(lg[j - 2][:, :T], p_[:, :T], [p_], [lg[j - 2]])
        CK()
        b, vw = load_panel(Win, 0, 8, 2048, 512)
        for j in range(2):
            p_ = PB()
            for kc in range(8):
                MM(p_, p_[:, :T], vw[:, kc, j * 128:(j + 1) * 128], hn[kc][:, :T], kc == 0, kc == 7, [b, hn[kc]])
            dst = gq if j == 0 else gk
            evac(dst[:, :T], p_[:, :T], [p_], [dst])
        for tt in range(NT):
            tw = min(128, T - tt * 128)
            p_ = PB()
            for kc in range(8):
                MM(p_, p_[:tw, 0:256], hn[kc][:, tt * 128:tt * 128 + tw], vw[:, kc, 256:512], kc == 0, kc == 7, [b, hn[kc]])
            for h in range(4):
                o0 = (tt * 4 + h) * 128 + (h % 2) * 64
                evac(gv[:tw, o0:o0 + 64], p_[:tw, h * 64:(h + 1) * 64], [p_], [gv], 'act' if tt % 2 else 'dve')
        CK()
        b, vw = load_panel(Win, 0, 8, 2560, 272)
        p_ = PB()
        for kc in range(8):
            MM(p_, p_[0:16, :T], vw[:, kc, 0:16], hn[kc][:, :T], kc == 0, kc == 7, [b, hn[kc]])
        evac(glr[0:16, :T], p_[0:16, :T], [p_], [glr])
        for j in range(2):
            p_ = PB()
            for kc in range(8):
                MM(p_, p_[:, :T], vw[:, kc, 16 + j * 128:16 + (j + 1) * 128], hn[kc][:, :T], kc == 0, kc == 7, [b, hn[kc]])
            evac(gog[j][:, :T], p_[:, :T], [p_], [gog[j]])

        CK()
        nlam = misc[:, 2:3]
        gsub = misc[:, 3:4]
        li = [0]
        LACC = os.environ.get('LACC', '1') == '1'
        for h in range(4):
            O = [ps[4], ps[5]]
            L = [ps[6], ps[7]]
            lacc = [tmp[4], tmp[5]]
            if LACC:
                for c in range(2):
                    C.op('pool', lambda e, c=c: e.memset(lacc[c][:, 0:512], 0.0), W=[lacc[c]])
            cinfo = []
            for (cb, nblk, nk_last, masked) in chunks:
                i = li[0] % 3
                li[0] += 1
                if masked:
                    blocks = [(hb * 64, 64, hb * 128, hb * 64) for hb in range(8)]
                elif nblk > 1:
                    blocks = [(bb * 128, 128, bb * 128, 0) for bb in range(nblk)]
                else:
                    blocks = [(0, nk_last, 0, 0)]
                cinfo.append((cb, nblk, nk_last, masked, i, blocks))

            def issue(ci):
                cb, nblk, nk_last, masked, i, blocks = cinfo[ci]
                kt_, vt_ = kld[i], vld[i]
                nkeys = (nblk - 1) * 128 + nk_last
                DMA(kt_[:, :nkeys], kT_d.t[h, :, cb * 128:cb * 128 + nkeys], [kT_d], [kt_], 'kl%d' % i)
                if masked:
                    DMA(vt_[0:64, :].rearrange("p (b e) -> p b e", b=8),
                        vv_d.t[cb * 128:(cb + 4) * 128, h * 128:(h + 1) * 128].rearrange("(b p) e -> p b e", p=64), [vv_d], [vt_], 'vl%d' % i)
                elif nblk > 1:
                    DMA(vt_[:, :nblk * 128].rearrange("p (b e) -> p b e", b=nblk),
                        vv_d.t[cb * 128:(cb + nblk) * 128, h * 128:(h + 1) * 128].rearrange("(b p) e -> p b e", p=128), [vv_d], [vt_], 'vl%d' % i)
                else:
                    DMA(vt_[:nk_last, 0:128], vv_d.t[cb * 128:cb * 128 + nk_last, h * 128:(h + 1) * 128], [vv_d], [vt_], 'vl%d' % i)

            flat = []
            for ci, inf in enumerate(cinfo):
                for bj, blk in enumerate(inf[5]):
                    flat.append((ci, bj == 0, inf[4], blk))
            nfl = len(flat)

            def emit_s(idx):
                ci, isfirst, i, (ko_, nk, vo_, q0) = flat[idx]
                if isfirst and ci + 1 < len(cinfo):
                    issue(ci + 1)
                kt_ = kld[i]
                for c in range(2):
                    S = ps[(idx % 2) * 2 + c]
                    P = pt[(idx % 2) * 2 + c]
                    MM(S, S[:nk, q0:T], kt_[:, ko_:ko_ + nk], qp[h][c][:, q0:T], True, True, [kt_, qp[h][c]])
                    ACT(P[:nk, q0:T], S[:nk, q0:T], AF.Exp, [S], [P], scale=0.125)

            def emit_pv(idx):
                ci, isfirst, i, (ko_, nk, vo_, q0) = flat[idx]
                vt_ = vld[i]
                for c in range(2):
                    P = pt[(idx % 2) * 2 + c]
                    MM(O[c], O[c][:, q0:T], vt_[:nk, vo_:vo_ + 128], P[:nk, q0:T], idx == 0, idx == nfl - 1, [vt_, P])
                    if LACC:
                        TT('dve', lacc[c][:nk, q0:T], lacc[c][:nk, q0:T], P[:nk, q0:T], ALU.add, [lacc[c], P], [lacc[c]])
                    else:
                        MM(L[c], L[c][:, q0:T], ones_b[:nk, :], P[:nk, q0:T], idx == 0, idx == nfl - 1, [ones_b, P])

            issue(0)
            for idx in range(nfl):
                emit_s(idx)
                if idx > 0:
                    emit_pv(idx - 1)
            emit_pv(nfl - 1)
            if LACC:
                for c in range(2):
                    MM(L[c], L[c][:, :T], ones_f[:, :], lacc[c][:, :T], True, True, [ones_f, lacc[c]])
            r1, t1, r2, t2 = tmp[0], tmp[1], tmp[2], tmp[3]
            C.op('dve', lambda e: e.reciprocal(r1[:, :T], L[0][:, :T]), R=[L[0]], W=[r1])
            TT('dve', t1[:, :T], O[0][:, :T], r1[:, :T], ALU.mult, [O[0], r1], [t1])
            C.op('dve', lambda e: e.reciprocal(r2[:, :T], L[1][:, :T]), R=[L[1]], W=[r2])
            TT('dve', t2[:, :T], O[1][:, :T], r2[:, :T], ALU.mult, [O[1], r2], [t2])
            STT(t1[:, :T], t2[:, :T], nlam, t1[:, :T], ALU.mult, ALU.add, [t2, t1, misc], [t1])
            TT('pool', sqb[0][:, :T], t1[:, :T], t1[:, :T], ALU.mult, [t1], [sqb[0]])
            MM(ps[0], ps[0][:, :T], ones_b[:, :], sqb[0][:, :T], True, True, [ones_b, sqb[0]])
            ACT(r1[:, :T], ps[0][:, :T], AF.Ln, [ps[0]], [r1], bias=EPS, scale=1.0 / 128)
            ACT(r1[:, :T], r1[:, :T], AF.Exp, [r1], [r1], scale=-0.5)
            STT(mix[h][:, :T], t1[:, :T], gsub, r1[:, :T], ALU.mult, ALU.mult, [t1, misc, r1], [mix[h]])

        CK()
        for j in range(2):
            xc, ea, ei, a_, om, u_, h_, g1 = tmp[0], tmp[1], tmp[2], tmp[3], tmp[4], tmp[5], tmp[6], tmp[7]
            lxj = lx[j]
            w0 = O_LCW + 4 * j
            TS('dve', xc[:, :T], lxj[:, 0:T], par[:, w0:w0 + 1], par[:, O_LCB + j:O_LCB + j + 1], ALU.mult, ALU.add, [lxj, par], [xc])
            for t_ in range(1, 4):
                STT(xc[:, :T], lxj[:, t_:t_ + T], par[:, w0 + t_:w0 + t_ + 1], xc[:, :T], ALU.mult, ALU.add, [lxj, par, xc], [xc])
            C.op('pool', lambda e, lxj=lxj: e.tensor_copy(misc[:, 8:11], lxj[:, T:T + 3]), R=[lxj], W=[misc])
            C.op('pool', lambda e, lxj=lxj: e.tensor_copy(lxj[:, 0:3], misc[:, 8:11]), R=[misc], W=[lxj])
            pa, pi_ = PB(), PB()
            MM(pa, pa[:, :T], wg[j][:, :], xc[:, :T], True, True, [wg[j], xc])
            MM(pi_, pi_[:, :T], wg[2 + j][:, :], xc[:, :T], True, True, [wg[2 + j], xc])
            ACT(ea[:, :T], pa[:, :T], AF.Exp, [pa, misc], [ea], bias=misc[:, 4 + j:5 + j], scale=-1.0)
            ACT(ei[:, :T], pi_[:, :T], AF.Exp, [pi_, misc], [ei], bias=misc[:, 6 + j:7 + j], scale=-1.0)
            sigmoid_inplace(ea[:, :T], ea)
            sigmoid_inplace(ei[:, :T], ei)
            ACT(a_[:, :T], ea[:, :T], AF.Exp, [ea, misc], [a_], scale=misc[:, 11 + j:12 + j])
            TT('pool', om[:, :T], a_[:, :T], a_[:, :T], ALU.mult, [a_], [om])
            ACT(om[:, :T], om[:, :T], AF.Ln, [om], [om], bias=1.0, scale=-1.0)
            ACT(om[:, :T], om[:, :T], AF.Exp, [om], [om], scale=0.5)
            TT('dve', u_[:, :T], ei[:, :T], xc[:, :T], ALU.mult, [ei, xc], [u_])
            TT('dve', u_[:, :T], u_[:, :T], om[:, :T], ALU.mult, [u_, om], [u_])
            C.op('dve', lambda e, j=j: e.tensor_tensor_scan(h_[:, :T], a_[:, :T], u_[:, :T], hprev[:, j:j + 1], ALU.mult, ALU.add),
                 R=[a_, u_, hprev], W=[h_])
            C.op('dve', lambda e, j=j: e.tensor_copy(hprev[:, j:j + 1], h_[:, T - 1:T]), R=[h_], W=[hprev])
            lgj = lg[j]
            TT('pool', g1[:, :T], lgj[:, :T], lgj[:, :T], ALU.mult, [lgj], [g1])
            TS('pool', g1[:, :T], g1[:, :T], 0.044715, 1.0, ALU.mult, ALU.add, [g1], [g1])
            TT('pool', g1[:, :T], g1[:, :T], lgj[:, :T], ALU.mult, [g1, lgj], [g1])
            ACT(g1[:, :T], g1[:, :T], AF.Exp, [g1], [g1], scale=-1.5957691216057308)
            sigmoid_inplace(g1[:, :T], g1)
            TT('dve', g1[:, :T], g1[:, :T], lgj[:, :T], ALU.mult, [g1, lgj], [g1])
            TT('dve', mix[4 + j][:, :T], g1[:, :T], h_[:, :T], ALU.mult, [g1, h_], [mix[4 + j]])

        CK()
        gg = tmp[8]
        p_ = PB()
        MM(p_, p_[:, :T], w2[0:16, :], glr[0:16, :T], True, True, [w2, glr])
        ACT(gg[:, :T], p_[:, :T], AF.Exp, [p_, misc], [gg], bias=misc[:, 13:14], scale=-1.0)
        ACT(gg[:, :T], gg[:, :T], AF.Ln, [gg], [gg], bias=1.0, scale=1.0)
        TS('dve', gg[:, :T], gg[:, :T], -1.0 / 16.0, None, ALU.mult, None, [gg], [gg])
        bc, eb, enb, eh = tmp[9], tmp[10], tmp[11], tmp[12]
        qs, ks, kh, aT = bt[0], bt[1], bt[2], bt[3]
        oT = [tmp[0], tmp[1]]
        C.op('pool', lambda e: e.memset(tmp[13][:, 0:128], 1.0), W=[tmp[13]])
        for ci in range((T + CT - 1) // CT):
            c0 = ci * CT
            sl = slice(c0, c0 + CT)
            C.op('dve', lambda e, sl=sl: e.tensor_tensor_scan(bc[:, :CT], tmp[13][:, :CT], gg[:, sl], 0.0, ALU.mult, ALU.add),
                 R=[tmp[13], gg], W=[bc])
            ACT(eb[:, :CT], bc[:, :CT], AF.Exp, [bc], [eb])
            ACT(enb[:, :CT], bc[:, :CT], AF.Exp, [bc], [enb], scale=-1.0)
            ACT(eh[:, :CT], bc[:, :CT], AF.Exp, [bc], [eh], bias=bc[:, CT - 1:CT], scale=-1.0)
            ACT(misc[:, 14:15], bc[:, CT - 1:CT], AF.Exp, [bc], [misc])
            STT(qs[:, :CT], gq[:, sl], 32.0 ** -0.5, eb[:, :CT], ALU.mult, ALU.mult, [gq, eb], [qs])
            TT('dve', ks[:, :CT], gk[:, sl], enb[:, :CT], ALU.mult, [gk, enb], [ks])
            TT('dve', kh[:, :CT], gk[:, sl], eh[:, :CT], ALU.mult, [gk, eh], [kh])
            pa = PB()
            for h in range(4):
                kz = pt[h]
                TS('pool', kz[:, :CT], ks[:, :CT], hmask[:, h:h + 1], None, ALU.mult, None, [ks, hmask], [kz])
                MM(pa, pa[:CT, h * 128:h * 128 + CT], kz[:, :CT], qs[:, :CT], True, True, [kz, qs])
            for h in range(4):
                TT('dve', aT[:CT, h * 128:h * 128 + CT], pa[:CT, h * 128:h * 128 + CT], tri[:CT, 0:CT], ALU.mult, [pa, tri], [aT])
            ptk = PB()
            C.op('pe', lambda e, ptk=ptk: e.transpose(ptk.t.bitcast(BF16)[:CT, 0:128], kh[:, :CT], identb[:, :]),
                 R=[kh, identb], W=[ptk])
            kht = sqb[1]
            evac(kht[:CT, 0:128], ptk.t.bitcast(BF16)[:CT, 0:128], [ptk], [kht])
            for pr in range(2):
                po = PB()
                for k2 in range(2):
                    h = pr * 2 + k2
                    o0 = (ci * 4 + h) * 128
                    MM(po, po[:, :CT], gv[:CT, o0:o0 + 128], aT[:CT, h * 128:h * 128 + CT], k2 == 0, False, [gv, aT])
                    MM(po, po[:, :CT], spad[h][:, :], qs[:, :CT], False, k2 == 1, [spad[h], qs])
                evac(oT[pr][:, sl], po[:, :CT], [po], [oT[pr]])
            pd = PB()
            gvu = sqb[0]
            for h in range(4):
                o0 = (ci * 4 + h) * 128 + (h % 2) * 64
                C.op('pool', lambda e, h=h, o0=o0: e.tensor_copy(gvu[:CT, h * 64:(h + 1) * 64], gv[:CT, o0:o0 + 64]), R=[gv], W=[gvu])
            MM(pd, pd[:, 0:256], kht[:CT, 0:128], gvu[:CT, 0:256], True, True, [kht, gvu])
            dS = tmp[2]
            TT('dve', dS[:, 0:256], pd[:, 0:256], bmask[:, :], ALU.mult, [pd, bmask], [dS])
            C.op('dve', lambda e: e.tensor_reduce(out=dS[:, 256:320], in_=dS[:, 0:256].rearrange("p (h e) -> p e h", h=4), axis=AX.X, op=ALU.add),
                 R=[dS], W=[dS])
            STT(S_all[:, :], S_all[:, :], misc[:, 14:15], dS[:, 256:320], ALU.mult, ALU.add, [S_all, misc, dS], [S_all])
            for h in range(4):
                TS('pool', spad[h][:, (h % 2) * 64:(h % 2) * 64 + 64], S_all[:, :], hmask[:, h:h + 1], None, ALU.mult, None,
                   [S_all, hmask], [spad[h]])
        for pr in range(2):
            o_ = oT[pr]
            TT('pool', sqb[0][:, :T], o_[:, :T], o_[:, :T], ALU.mult, [o_], [sqb[0]])
            p_ = PB()
            MM(p_, p_[:, :T], bones[:, :], sqb[0][:, :T], True, True, [bones, sqb[0]])
            rs = tmp[2]
            ACT(rs[:, :T], p_[:, :T], AF.Ln, [p_], [rs], bias=EPS, scale=1.0 / 64)
            ACT(rs[:, :T], rs[:, :T], AF.Exp, [rs], [rs], scale=-0.5)
            STT(o_[:, :T], o_[:, :T], par[:, O_GNG:O_GNG + 1], rs[:, :T], ALU.mult, ALU.mult, [o_, par, rs], [o_])
            sg = tmp[3]
            ACT(sg[:, :T], gog[pr][:, :T], AF.Exp, [gog[pr]], [sg], scale=-1.0)
            sigmoid_inplace(sg[:, :T], sg)
            TT('dve', sg[:, :T], sg[:, :T], gog[pr][:, :T], ALU.mult, [sg, gog[pr]], [sg])
            TT('dve', mix[6 + pr][:, :T], sg[:, :T], o_[:, :T], ALU.mult, [sg, o_], [mix[6 + pr]])

        CK()
        for half in range(2):
            b, vw = load_panel(Wout, 0, 8, half * 512, 512)
            for f4 in range(4):
                f = half * 4 + f4
                p_ = PB()
                for kc in range(8):
                    MM(p_, p_[:, :T], vw[:, kc, f4 * 128:(f4 + 1) * 128], mix[kc][:, :T], kc == 0, kc == 7, [b, mix[kc]])
                TT('dve', x[f][:, :T], p_[:, :T], x[f][:, :T], ALU.add, [p_, x[f]], [x[f]])
        CK()
        norm_to_hn(T, par, O_G2)
        for half in range(2):
            fl = 0
            for c0, ncols in ((0, 512), (512, 512), (1024, 384)):
                cc = half * 1408 + c0
                bu, vu = load_panel(Wup, 0, 8, cc, ncols)
                bg, vg = load_panel(Wgt, 0, 8, cc, ncols)
                for f4 in range(ncols // 128):
                    f = half * 11 + fl
                    pu, pg = PB(), PB()
                    for kc in range(8):
                        MM(pu, pu[:, :T], vu[:, kc, f4 * 128:(f4 + 1) * 128], hn[kc][:, :T], kc == 0, kc == 7, [bu, hn[kc]])
                    for kc in range(8):
                        MM(pg, pg[:, :T], vg[:, kc, f4 * 128:(f4 + 1) * 128], hn[kc][:, :T], kc == 0, kc == 7, [bg, hn[kc]])
                    up = tmp[4 + (fl % 2) * 4]
                    uc = tmp[5 + (fl % 2) * 4]
                    g1 = tmp[6 + (fl % 2) * 4]
                    C.op('act', lambda e, up=up, pu=pu: e.activation(out=up[:, 2:2 + T], in_=pu[:, :T], func=AF.Copy), R=[pu], W=[up])
                    C.op('pool', lambda e, up=up, f=f: e.tensor_copy(up[:, 0:2], fst[:, 2 * f:2 * f + 2]), R=[fst], W=[up])
                    C.op('pool', lambda e, up=up, f=f: e.tensor_copy(fst[:, 2 * f:2 * f + 2], up[:, T:T + 2]), R=[up], W=[fst])
                    wc = O_FCW + 3 * f
                    TS('dve', uc[:, :T], up[:, 0:T], par[:, wc:wc + 1], par[:, O_FCB + f:O_FCB + f + 1], ALU.mult, ALU.add, [up, par], [uc])
                    STT(uc[:, :T], up[:, 1:1 + T], par[:, wc + 1:wc + 2], uc[:, :T], ALU.mult, ALU.add, [up, par, uc], [uc])
                    STT(uc[:, :T], up[:, 2:2 + T], par[:, wc + 2:wc + 3], uc[:, :T], ALU.mult, ALU.add, [up, par, uc], [uc])
                    TT('pool', g1[:, :T], uc[:, :T], uc[:, :T], ALU.mult, [uc], [g1])
                    TS('pool', g1[:, :T], g1[:, :T], 0.044715, 1.0, ALU.mult, ALU.add, [g1], [g1])
                    TT('pool', g1[:, :T], g1[:, :T], uc[:, :T], ALU.mult, [g1, uc], [g1])
                    ACT(g1[:, :T], g1[:, :T], AF.Exp, [g1], [g1], scale=-1.5957691216057308)
                    sigmoid_inplace(g1[:, :T], g1)
                    TT('pool', g1[:, :T], g1[:, :T], uc[:, :T], ALU.mult, [g1, uc], [g1])
                    TT('dve', act[fl][:, :T], pg[:, :T], g1[:, :T], ALU.mult, [pg, g1], [act[fl]])
                    fl += 1
            for hc in range(2):
                b, vw = load_panel(Wdn, half * 1408, 11, hc * 512, 512)
                for f4 in range(4):
                    f = hc * 4 + f4
                    p_ = PB()
                    for kc in range(11):
                        MM(p_, p_[:, :T], vw[:, kc, f4 * 128:(f4 + 1) * 128], act[kc][:, :T], kc == 0, kc == 10, [b, act[kc]])
                    TT('dve', x[f][:, :T], p_[:, :T], x[f][:, :T], ALU.add, [p_, x[f]], [x[f]])
        CK()
        if not last:
            for kc in range(8):
                DMA(xdst[1][kc * 128:(kc + 1) * 128, :], x[kc][:, :T], [x[kc]], [xdst[0]], 'xs', q='pool')
        else:
            rmsnorm_stats(T)
            for kc in range(8):
                yo = tmp[kc % 4]
                STT(yo[:, :T], x[kc][:, :T], gfin[:, kc:kc + 1], tmp[13][:, :T], ALU.mult, ALU.mult, [x[kc], gfin, tmp[13]], [yo])
                DMA(ydst[1][kc * 128:(kc + 1) * 128, :], yo[:, :T], [yo], [ydst[0]], 'yo%d' % (kc % 4), q='pool')

    def layer_setup(l):
        DMA(par[:], par_in.t[l, :, :], [], [par], 'par')
        for i in range(4):
            DMA(wg[i][:], wg_in.t[l, i, :, :], [], [wg[i]], 'par')
        DMA(w2[:], w2_in.t[l, :, :], [], [w2], 'par')
        lam_init = 0.8 - 0.6 * math.exp(-0.3 * l)
        al = par[:, O_AL:O_AL + 256]
        TT('dve', tmp[0][:, 0:64], par[:, O_AL:O_AL + 64], par[:, O_AL + 64:O_AL + 128], ALU.mult, [par], [tmp[0]])
        TT('dve', tmp[0][:, 64:128], par[:, O_AL + 128:O_AL + 192], par[:, O_AL + 192:O_AL + 256], ALU.mult, [par], [tmp[0]])
        C.op('dve', lambda e: e.tensor_reduce(out=misc[:, 0:1], in_=tmp[0][:, 0:64], axis=AX.X, op=ALU.add), R=[tmp[0]], W=[misc])
        C.op('dve', lambda e: e.tensor_reduce(out=misc[:, 1:2], in_=tmp[0][:, 64:128], axis=AX.X, op=ALU.add), R=[tmp[0]], W=[misc])
        ACT(misc[:, 0:2], misc[:, 0:2], AF.Exp, [misc], [misc])
        TT('dve', misc[:, 2:3], misc[:, 1:2], misc[:, 0:1], ALU.subtract, [misc], [misc])
        TS('dve', misc[:, 2:3], misc[:, 2:3], -lam_init, None, ALU.add, None, [misc], [misc])
        TS('dve', misc[:, 3:4], par[:, O_SUBG:O_SUBG + 1], 1.0 - lam_init, None, ALU.mult, None, [par], [misc])
        TS('dve', misc[:, 4:6], par[:, O_BA:O_BA + 2], -1.0, None, ALU.mult, None, [par], [misc])
        TS('dve', misc[:, 6:8], par[:, O_BX:O_BX + 2], -1.0, None, ALU.mult, None, [par], [misc])
        TS('dve', misc[:, 13:14], par[:, O_GGB:O_GGB + 1], -1.0, None, ALU.mult, None, [par], [misc])
        ACT(misc[:, 11:13], par[:, O_LAM:O_LAM + 2], AF.Exp, [par], [misc], scale=-1.0)
        ACT(misc[:, 11:13], misc[:, 11:13], AF.Ln, [misc], [misc], bias=1.0)
        TS('dve', misc[:, 11:13], misc[:, 11:13], -8.0, None, ALU.mult, None, [misc], [misc])

    def state_zero():
        C.op('pool', lambda e: e.memset(hprev[:], 0.0), W=[hprev])
        C.op('pool', lambda e: e.memset(S_all[:], 0.0), W=[S_all])
        C.op('pool', lambda e: e.memset(fst[:], 0.0), W=[fst])
        for j in range(2):
            C.op('pool', lambda e, j=j: e.memset(lx[j][:, 0:3], 0.0), W=[lx[j]])
        for h in range(4):
            C.op('pool', lambda e, h=h: e.memset(spad[h][:], 0.0), W=[spad[h]])

    def state_load(l, s):
        DMA(hprev[:], s_lh.t[l, s, :, :], [], [hprev], 'st')
        DMA(S_all[:], s_gla.t[l, s, :, :], [], [S_all], 'st')
        DMA(fst[:], s_fc.t[l, s, :, :], [], [fst], 'st')
        DMA(tmp[0][:, 0:6], s_lc.t[l, s, :, :], [], [tmp[0]], 'st')
        for j in range(2):
            C.op('pool', lambda e, j=j: e.tensor_copy(lx[j][:, 0:3], tmp[0][:, 3 * j:3 * j + 3]), R=[tmp[0]], W=[lx[j]])
        for h in range(4):
            TS('pool', spad[h][:, (h % 2) * 64:(h % 2) * 64 + 64], S_all[:, :], hmask[:, h:h + 1], None, ALU.mult, None,
               [S_all, hmask], [spad[h]])

    def state_store(lh_ap, lc_ap, gla_ap, fc_ap, obuf):
        DMA(lh_ap, hprev[:], [hprev], [obuf], 'sto', q='pool')
        DMA(gla_ap, S_all[:], [S_all], [obuf], 'sto', q='pool')
        DMA(fc_ap, fst[:], [fst], [obuf], 'sto', q='pool')
        for j in range(2):
            DMA(lc_ap[:, 3 * j:3 * j + 3], lx[j][:, 0:3], [lx[j]], [obuf], 'sto', q='pool')

    try:
        for l in range(DEPTH):
            for (src, dst, rows, cols) in ((w_in, wb_in, D, INW), (w_out, wb_out, D, D), (w_up, wb_up, D, FF),
                                           (w_gt, wb_gt, D, FF), (w_dn, wb_dn, FF, D)):
                convert(src, dst, l, rows, cols)
        C.gbar = [('cvs%d' % i, C.dcnt['cvs%d' % i]) for i in range(4)]

        obuf = C.dram("dummy_track", [1, 1], F32)
        for l in range(DEPTH):
            layer_setup(l)
            state_zero()
            last = l == DEPTH - 1
            xs_src = xT_in if l == 0 else xsc
            group_layer(l, NM, (xs_src, xs_src.t[:, 0:NM]), (xsc, xsc.t[:, 0:NM]), (yT, yT.t[:, 0:NM]), kT_p, vv_p, 0, 0,
                        [(0, 1, NM, False)], (k_p, k_p.t[l, 0:NM, :]), (v_p, v_p.t[l, 0:NM, :]), last)
            for g in range(NG):
                t0 = NM + 512 * g
                chunks = [(0, 1, NM, False)] + [(1 + 4 * j, 4, 128, False) for j in range(g)] + [(1 + 4 * g, 4, 128, True)]
                group_layer(l, 512, (xs_src, xs_src.t[:, t0:t0 + 512]), (xsc, xsc.t[:, t0:t0 + 512]), (yT, yT.t[:, t0:t0 + 512]),
                            kT_p, vv_p, 1 + 4 * g, 0, chunks, (k_p, k_p.t[l, t0:t0 + 512, :]), (v_p, v_p.t[l, t0:t0 + 512, :]), last)
            state_store(lh_p.t[l, :, :], lc_p.t[l, :, :], gla_p.t[l, :, :], fc_p.t[l, :, :], obuf)
        for l in range(DEPTH):
            layer_setup(l)
            last = l == DEPTH - 1
            for s in range(2):
                for bb in range(PAST // 128):
                    i = bb % 2
                    kc_, vc_ = tmp[4 + i], tmp[6 + i]
                    DMA(kc_[:, 0:512], ck_in.t[l, s, bb * 128:(bb + 1) * 128, :], [], [kc_], 'ckl%d' % i)
                    DMA(vc_[:, 0:512], cv_in.t[l, s, bb * 128:(bb + 1) * 128, :], [], [vc_], 'cvl2%d' % i)
                    p_ = PB()
                    for h in range(4):
                        C.op('pe', lambda e, p_=p_, kc_=kc_, h=h: e.transpose(p_[:, h * 128:(h + 1) * 128], kc_[:, h * 128:(h + 1) * 128], ident[:, :]),
                             R=[kc_, ident], W=[p_])
                    kb_, vb_ = bt[i], bt[2 + i]
                    evac(kb_[:, :], p_[:, :], [p_], [kb_])
                    C.op('pool', lambda e, vb_=vb_, vc_=vc_: e.tensor_copy(vb_[:, :], vc_[:, 0:512]), R=[vc_], W=[vb_])
                    DMA(kT_s[s].t[:, :, bb * 128:(bb + 1) * 128].rearrange("h p k -> p h k"),
                        kb_[:, :].rearrange("p (h k) -> p h k", h=4), [kb_], [kT_s[s]], 'cks', q='pool')
                    DMA(vv_s[s].t[bb * 128:(bb + 1) * 128, :], vb_[:, :], [vb_], [vv_s[s]], 'cvs2', q='pool')
                state_load(l, s)
                nb = PAST // 128
                chunks = [(4 * j, 4, 128, False) for j in range(nb // 4)] + [(nb, 1, DS, False)]
                xs_src = xsT_in if l == 0 else xssc
                group_layer(l, DS, (xs_src, xs_src.t[s, :, :]), (xssc, xssc.t[s, :, :]), (ysT, ysT.t[s, :, :]), kT_s[s], vv_s[s], nb, 0,
                            chunks, (k_s, k_s.t[l, s, :, :]), (v_s, v_s.t[l, s, :, :]), last)
                state_store(lh_s.t[l, s, :, :], lc_s.t[l, s, :, :], gla_s.t[l, s, :, :], fc_s.t[l, s, :, :], obuf)

    except _Stop:
        pass
    print('SBUF remaining', nc.sbuf_bytes_remaining, flush=True)
    C.emit()
    es.close()
    return nc


_NC_CACHE = {}


def _consts():
    p = np.arange(128)
    q = np.arange(512)
    masks = np.zeros((4, 128, 512), np.float32)
    for kb in range(4):
        masks[kb] = (((kb * 128 + p) // 64)[:, None] <= (q // 64)[None, :]).astype(np.float32)
    tri = np.zeros((128, 512), np.float32)
    tri[:, :128] = (p[:, None] <= p[None, :]).astype(np.float32)
    bones = ((p // 64)[:, None] == (p // 64)[None, :]).astype(np.float32)
    hmask = ((p // 32)[:, None] == np.arange(4)[None, :]).astype(np.float32)
    bmask = ((p // 32)[:, None] == (np.arange(256) // 64)[None, :]).astype(np.float32)
    return dict(c_ident=np.eye(128, dtype=np.float32), c_masks=masks, c_tri=tri, c_bones=bones, c_hmask=hmask, c_bmask=bmask)


def kernel(**inp):
    f = lambda k: np.ascontiguousarray(np.asarray(inp[k], dtype=np.float32))
    x_prompt, x_sample = f('x_prompt'), f('x_sample')
    B, SEQ, _ = x_prompt.shape
    NSB = x_sample.shape[0]
    ck, cv = f('cache_attn_k'), f('cache_attn_v')
    DEPTH, _, PAST = ck.shape[0], ck.shape[1], ck.shape[2]
    LT = NM + SEQ
    key = (SEQ, DEPTH, PAST)
    if key not in _NC_CACHE:
        _NC_CACHE[key] = build(SEQ, DEPTH, PAST)
    nc = _NC_CACHE[key]
    meta = f('meta_tokens')
    par = np.zeros((DEPTH, 128, NP), np.float32)
    colmaj = lambda a, n: a.reshape(DEPTH, n, 128).transpose(0, 2, 1)
    par[:, :, O_G1:O_G1 + 8] = colmaj(f('norm_mix_g'), 8)
    par[:, :, O_G2:O_G2 + 8] = colmaj(f('norm_ffn_g'), 8)
    lcw = f('lru_conv_w')
    par[:, :, O_LCW:O_LCW + 8] = lcw.reshape(DEPTH, 4, 2, 128).transpose(0, 3, 2, 1).reshape(DEPTH, 128, 8)
    par[:, :, O_LCB:O_LCB + 2] = colmaj(f('lru_conv_b'), 2)
    par[:, :, O_BA:O_BA + 2] = colmaj(f('lru_gate_a_b'), 2)
    par[:, :, O_BX:O_BX + 2] = colmaj(f('lru_gate_x_b'), 2)
    par[:, :, O_LAM:O_LAM + 2] = colmaj(f('lru_log_lambda'), 2)
    par[:, :, O_GGB] = f('gla_gate_b')
    par[:, :, O_GNG] = np.tile(f('gla_norm_g'), (1, 2))
    par[:, :, O_SUBG] = f('attn_subln_g')
    fcw = f('ffn_conv_w')
    par[:, :, O_FCW:O_FCW + 66] = fcw.reshape(DEPTH, 3, 22, 128).transpose(0, 3, 2, 1).reshape(DEPTH, 128, 66)
    par[:, :, O_FCB:O_FCB + 22] = colmaj(f('ffn_conv_b'), 22)
    par[:, :, O_AL:O_AL + 256] = f('attn_lambda').reshape(DEPTH, 1, 256)
    gfin = f('norm_final_g').reshape(8, 128).T.copy()
    wgate = np.zeros((DEPTH, 4, 128, 128), np.float32)
    for gi, nm in enumerate(('lru_gate_a_w', 'lru_gate_x_w')):
        w = f(nm)
        for n in range(8):
            j, r = n // 4, (n % 4) * 32
            wgate[:, gi * 2 + j, r:r + 32, r:r + 32] = w[:, n]
    shared = dict(par=par, gfin=gfin, wgate=wgate, w2=f('gla_gate_w2'), w_in=f('w_in'), w_out=f('w_out'), w_up=f('ffn_w_up'),
                  w_gt=f('ffn_w_gate'), w_dn=f('ffn_w_down'))
    shared.update(_consts())
    slh, slc, sgl, sfc = f('state_lru_h'), f('state_lru_conv'), f('state_gla'), f('state_ffn_conv')
    in_maps = []
    for c in range(8):
        bp = c % B
        ss = [(2 * c) % NSB, (2 * c + 1) % NSB]
        m = dict(shared)
        m['xT_in'] = np.ascontiguousarray(np.concatenate([meta, x_prompt[bp]], 0).T)
        m['xsT_in'] = np.ascontiguousarray(x_sample[ss].transpose(0, 2, 1))
        m['ck'] = np.ascontiguousarray(ck[:, ss].reshape(DEPTH, 2, PAST, 512))
        m['cv'] = np.ascontiguousarray(cv[:, ss].reshape(DEPTH, 2, PAST, 512))
        m['s_lh'] = np.ascontiguousarray(slh[:, ss].reshape(DEPTH, 2, 2, 128).transpose(0, 1, 3, 2))
        m['s_lc'] = np.ascontiguousarray(slc[:, ss].reshape(DEPTH, 2, 3, 2, 128).transpose(0, 1, 4, 3, 2).reshape(DEPTH, 2, 128, 6))
        m['s_gla'] = np.ascontiguousarray(sgl[:, ss].reshape(DEPTH, 2, 128, 64))
        m['s_fc'] = np.ascontiguousarray(sfc[:, ss].reshape(DEPTH, 2, 2, 22, 128).transpose(0, 1, 4, 3, 2).reshape(DEPTH, 2, 128, 44))
        in_maps.append(m)
    res = run_bass_kernel_spmd(nc, in_maps, core_ids=list(range(8)))
    R = res.results
    _NC_CACHE['last_results'] = R
    y_prompt = np.stack([R[b]['yT'].T[NM:] for b in range(B)])
    k_p = np.stack([R[b]['k_p'] for b in range(B)], 1).reshape(DEPTH, B, LT, 4, 128)
    v_p = np.stack([R[b]['v_p'] for b in range(B)], 1).reshape(DEPTH, B, LT, 4, 128)
    lh_p = np.stack([R[b]['lh_p'].transpose(0, 2, 1).reshape(DEPTH, 256) for b in range(B)], 1)
    lc_p = np.stack([R[b]['lc_p'].reshape(DEPTH, 128, 2, 3).transpose(0, 3, 2, 1).reshape(DEPTH, 3, 256) for b in range(B)], 1)
    gla_p = np.stack([R[b]['gla_p'].reshape(DEPTH, 4, 32, 64) for b in range(B)], 1)
    fc_p = np.stack([R[b]['fc_p'].reshape(DEPTH, 128, 22, 2).transpose(0, 3, 2, 1).reshape(DEPTH, 2, FF) for b in range(B)], 1)
    ncs = NSB // 2
    cat = lambda k, ax: np.concatenate([R[c][k] for c in range(ncs)], ax)
    y_sample = cat('ysT', 0).transpose(0, 2, 1)
    k_s = cat('k_s', 1).reshape(DEPTH, NSB, DS, 4, 128)
    v_s = cat('v_s', 1).reshape(DEPTH, NSB, DS, 4, 128)
    lh_s = cat('lh_s', 1).transpose(0, 1, 3, 2).reshape(DEPTH, NSB, 256)
    lc_s = cat('lc_s', 1).reshape(DEPTH, NSB, 128, 2, 3).transpose(0, 1, 4, 3, 2).reshape(DEPTH, NSB, 3, 256)
    gla_s = cat('gla_s', 1).reshape(DEPTH, NSB, 4, 32, 64)
    fc_s = cat('fc_s', 1).reshape(DEPTH, NSB, 128, 22, 2).transpose(0, 1, 4, 3, 2).reshape(DEPTH, NSB, 2, FF)
    outs = (y_prompt, y_sample, k_p, v_p, lh_p, lc_p, gla_p, fc_p, k_s, v_s, lh_s, lc_s, gla_s, fc_s)
    return tuple(np.ascontiguousarray(o, dtype=np.float32) for o in outs)
```
